# Optimizing a Trainium2 kernel written in Bass

```python
import math
import jax, jax.numpy as jnp
from jax import lax
import numpy as np

D_MODEL = 4096
BATCH = 2
SEQ = 8192
DEPTH = 1

N_HEADS = 16
HEAD_DIM = 64
ATTN_WIDTH = N_HEADS * 2 * HEAD_DIM
SSM_WIDTH = D_MODEL // 2
SSM_GROUP = 16
N_GROUPS = SSM_WIDTH // SSM_GROUP
STATE = 64
D_FF = 11008
CONV_W = 3
Q_BLOCK = 128
SCAN_CHUNK = 128
EPS = 1e-6
IN_SPLITS = [ATTN_WIDTH, 2 * ATTN_WIDTH, 3 * ATTN_WIDTH, 3 * ATTN_WIDTH + SSM_WIDTH,
             3 * ATTN_WIDTH + SSM_WIDTH + D_MODEL]
IN_WIDTH = 3 * ATTN_WIDTH + SSM_WIDTH + 2 * D_MODEL

kernel_name = "hybrid_diffattn_s5_convffn_adaln"


def rmsnorm(x, g):
    xf = x.astype(jnp.float32)
    y = xf * lax.rsqrt(jnp.mean(xf * xf, axis=-1, keepdims=True) + EPS)
    return (y * g.astype(jnp.float32)).astype(x.dtype)


def diff_attention(q, k, v, lq1, lk1, lq2, lk2, subln_g, lam_init):
    B, L = q.shape[0], q.shape[1]
    nb = L // Q_BLOCK
    f32 = jnp.float32
    scale = HEAD_DIM ** -0.5
    lam = (jnp.exp(jnp.sum(lq1.astype(f32) * lk1.astype(f32)))
           - jnp.exp(jnp.sum(lq2.astype(f32) * lk2.astype(f32))) + lam_init)
    slopes = jnp.exp2(-8.0 * jnp.arange(1, N_HEADS + 1, dtype=f32) / N_HEADS)
    qb = q.astype(f32).reshape(B, nb, Q_BLOCK, N_HEADS, 2, HEAD_DIM).transpose(1, 0, 3, 4, 2, 5)
    kf = k.astype(f32).transpose(0, 2, 3, 1, 4)
    vf = v.astype(f32).transpose(0, 2, 1, 3)
    kpos = jnp.arange(L)

    def block(args):
        qblk, start = args
        s = jnp.einsum('bhjqd,bhjkd->bhjqk', qblk, kf) * scale
        qpos = start + jnp.arange(Q_BLOCK)
        dist = (qpos[:, None] - kpos[None, :]).astype(f32)
        bias = -slopes[:, None, None, None] * dist
        s = jnp.where(dist >= 0, s + bias, -jnp.inf)
        p = jax.nn.softmax(s, axis=-1)
        w = p[:, :, 0] - lam * p[:, :, 1]
        return jnp.einsum('bhqk,bhke->bhqe', w, vf)

    o = lax.map(block, (qb, jnp.arange(nb) * Q_BLOCK))
    o = rmsnorm(o, subln_g) * (1.0 - lam_init)
    return o.transpose(1, 0, 3, 2, 4).reshape(B, L, ATTN_WIDTH).astype(q.dtype)


def _ssm_combine(e1, e2):
    a1, b1 = e1
    a2, b2 = e2
    return a1 * a2, a2 * b1 + b2


def s5_branch(u, a_re, a_im, b_re, b_im, c_re, c_im, d_skip, log_dt, w_glu):
    B, L = u.shape[0], u.shape[1]
    f32 = jnp.float32
    nc = L // SCAN_CHUNK
    uf = u.astype(f32)
    u_chunks = uf.reshape(B, nc, SCAN_CHUNK, N_GROUPS, SSM_GROUP).transpose(1, 0, 2, 3, 4)
    lam = lax.complex(a_re.astype(f32), a_im.astype(f32))
    dt = jnp.exp(log_dt.astype(f32))[:, None]
    lam_dt = lam * dt
    a_bar = jnp.exp(lam_dt)
    b_mat = lax.complex(b_re.astype(f32), b_im.astype(f32))
    b_bar = ((a_bar - 1.0) / lam)[..., None] * b_mat
    c_mat = lax.complex(c_re.astype(f32), c_im.astype(f32))
    a_pow = jnp.exp(lam_dt[None] * jnp.arange(1, SCAN_CHUNK + 1, dtype=f32)[:, None, None])

    def chunk(carry, u_c):
        bu = jnp.einsum('gpc,btgc->btgp', b_bar, u_c.astype(jnp.complex64))
        a = jnp.broadcast_to(a_bar, bu.shape)
        _, xs = lax.associative_scan(_ssm_combine, (a, bu), axis=1)
        xs = xs + a_pow[None] * carry[:, None]
        y = jnp.einsum('gcp,btgp->btgc', c_mat, xs).real
        return xs[:, -1], y

    carry0 = jnp.zeros((B, N_GROUPS, STATE), jnp.complex64)
    _, y = lax.scan(chunk, carry0, u_chunks)
    y = y.transpose(1, 0, 2, 3, 4).reshape(B, L, SSM_WIDTH) + d_skip.astype(f32) * uf
    z = jax.nn.gelu(y)
    out = z * jax.nn.sigmoid(z @ w_glu.astype(f32))
    return out.astype(u.dtype)


def conv_ffn(h, w_up, conv_w, conv_b, w_down):
    L = h.shape[1]
    up = h @ w_up
    pad = jnp.pad(up, ((0, 0), (CONV_W - 1, 0), (0, 0)))
    conv = conv_b
    for j in range(CONV_W):
        conv = conv + conv_w[j] * pad[:, j:j + L]
    a, g = jnp.split(conv, 2, axis=-1)
    return (jax.nn.silu(g) * a) @ w_down


def setup_inputs(seed: int = 0) -> dict:
    key = jax.random.key(seed)
    ks = jax.random.split(key, 32)
    f32 = jnp.float32

    def nrm(k, shape, s):
        return jax.random.normal(k, shape, f32) * s

    P, G, c = STATE, N_GROUPS, SSM_GROUP
    return {
        "x": nrm(ks[0], (BATCH, SEQ, D_MODEL), 1.0),
        "c": nrm(ks[1], (BATCH, D_MODEL), 1.0),
        "ada_w": nrm(ks[2], (DEPTH, D_MODEL, 6 * D_MODEL), 0.5 * D_MODEL ** -0.5),
        "ada_b": nrm(ks[3], (DEPTH, 6 * D_MODEL), 0.01),
        "norm1_g": 1.0 + nrm(ks[4], (DEPTH, D_MODEL), 0.02),
        "w_in": nrm(ks[5], (DEPTH, D_MODEL, IN_WIDTH), D_MODEL ** -0.5),
        "lq1": nrm(ks[6], (DEPTH, HEAD_DIM), 0.1),
        "lk1": nrm(ks[7], (DEPTH, HEAD_DIM), 0.1),
        "lq2": nrm(ks[8], (DEPTH, HEAD_DIM), 0.1),
        "lk2": nrm(ks[9], (DEPTH, HEAD_DIM), 0.1),
        "subln_g": 1.0 + nrm(ks[10], (DEPTH, 2 * HEAD_DIM), 0.02),
        "a_re": -0.5 + nrm(ks[11], (DEPTH, G, P), 0.01),
        "a_im": math.pi * jnp.arange(P, dtype=f32) + nrm(ks[12], (DEPTH, G, P), 0.01),
        "b_re": nrm(ks[13], (DEPTH, G, P, c), (2 * c) ** -0.5),
        "b_im": nrm(ks[14], (DEPTH, G, P, c), (2 * c) ** -0.5),
        "c_re": nrm(ks[15], (DEPTH, G, c, P), (2 * P) ** -0.5 * 4.0),
        "c_im": nrm(ks[16], (DEPTH, G, c, P), (2 * P) ** -0.5 * 4.0),
        "d_skip": nrm(ks[17], (DEPTH, SSM_WIDTH), 1.0),
        "log_dt": jax.random.uniform(ks[18], (DEPTH, G), f32, math.log(0.001), math.log(0.1)),
        "w_glu": nrm(ks[19], (DEPTH, SSM_WIDTH, SSM_WIDTH), SSM_WIDTH ** -0.5),
        "w_attn_br": nrm(ks[20], (DEPTH, ATTN_WIDTH, D_MODEL), ATTN_WIDTH ** -0.5),
        "w_ssm_br": nrm(ks[21], (DEPTH, SSM_WIDTH, D_MODEL), SSM_WIDTH ** -0.5),
        "w_out": nrm(ks[22], (DEPTH, D_MODEL, D_MODEL), D_MODEL ** -0.5),
        "norm2_g": 1.0 + nrm(ks[23], (DEPTH, D_MODEL), 0.02),
        "w_up": nrm(ks[24], (DEPTH, D_MODEL, 2 * D_FF), D_MODEL ** -0.5),
        "conv_w": nrm(ks[25], (DEPTH, CONV_W, 2 * D_FF), CONV_W ** -0.5),
        "conv_b": nrm(ks[26], (DEPTH, 2 * D_FF), 0.01),
        "w_down": nrm(ks[27], (DEPTH, D_FF, D_MODEL), D_FF ** -0.5),
        "final_g": 1.0 + nrm(ks[28], (D_MODEL,), 0.02),
    }


def reference(x, c, ada_w, ada_b, norm1_g, w_in, lq1, lk1, lq2, lk2, subln_g,
              a_re, a_im, b_re, b_im, c_re, c_im, d_skip, log_dt, w_glu,
              w_attn_br, w_ssm_br, w_out, norm2_g, w_up, conv_w, conv_b, w_down,
              final_g):
    B, L = x.shape[0], x.shape[1]
    for l in range(DEPTH):
        lam_init = 0.8 - 0.6 * math.exp(-0.3 * l)
        mod = (c @ ada_w[l] + ada_b[l])[:, None, :]
        sh1, sc1, g1, sh2, sc2, g2 = jnp.split(mod, 6, axis=-1)

        h = rmsnorm(x, norm1_g[l]) * (1.0 + sc1) + sh1
        proj = h @ w_in[l]
        q, k, v, u, ga, gs = jnp.split(proj, IN_SPLITS, axis=-1)
        q = q.reshape(B, L, N_HEADS, 2, HEAD_DIM)
        k = k.reshape(B, L, N_HEADS, 2, HEAD_DIM)
        v = v.reshape(B, L, N_HEADS, 2 * HEAD_DIM)
        attn = diff_attention(q, k, v, lq1[l], lk1[l], lq2[l], lk2[l], subln_g[l], lam_init)
        ssm = s5_branch(u, a_re[l], a_im[l], b_re[l], b_im[l], c_re[l], c_im[l],
                        d_skip[l], log_dt[l], w_glu[l])
        merged = (jax.nn.sigmoid(ga) * (attn @ w_attn_br[l])
                  + jax.nn.sigmoid(gs) * (ssm @ w_ssm_br[l]))
        x = x + g1 * (merged @ w_out[l])

        h2 = rmsnorm(x, norm2_g[l]) * (1.0 + sc2) + sh2
        x = x + g2 * conv_ffn(h2, w_up[l], conv_w[l], conv_b[l], w_down[l])
    return rmsnorm(x, final_g)
```

```python
import numpy as np
from contextlib import ExitStack
import concourse.bass as bass
import concourse.mybir as mybir
from concourse.bass_utils import run_bass_kernel_spmd

F32 = mybir.dt.float32
BF16 = mybir.dt.bfloat16
AF = mybir.ActivationFunctionType
ALU = mybir.AluOpType


class Buf:
    __slots__ = ("t", "w", "r", "dsem", "dcnt", "name")

    def __init__(self, t, name):
        self.t = t
        self.w = None
        self.r = {}
        self.dsem = None
        self.dcnt = 0
        self.name = name

    def __getitem__(self, idx):
        return self.t[idx]


class Ctx:
    def __init__(self, nc, es):
        self.nc = nc
        self.es = es
        self.E = {"pe": nc.tensor, "act": nc.scalar, "dve": nc.vector,
                  "pool": nc.gpsimd, "sp": nc.sync}
        self.sem = {}
        self.cnt = {}
        for e in ("pe", "act", "dve", "pool"):
            self.sem[e] = es.enter_context(nc.semaphore("s_" + e))
            self.cnt[e] = 0
        self.waited = {}
        self.nbuf = 0
        self.dbufs = []

    def sb(self, name, shape, dt):
        return Buf(self.es.enter_context(self.nc.sbuf_tensor(name + "_s", list(shape), dt)), name)

    def ps(self, name, shape, dt=F32):
        return Buf(self.es.enter_context(self.nc.psum_tensor(name, list(shape), dt)), name)

    def dram(self, name, shape, dt):
        return Buf(self.nc.dram_tensor(name, list(shape), dt).ap(), name)

    def ext(self, name, shape, dt, out=False):
        return Buf(self.nc.dram_tensor(name, list(shape), dt,
                                       kind="ExternalOutput" if out else "ExternalInput").ap(), name)

    def _wait(self, e, ticks, force_self=False):
        for (sem, val, key) in ticks:
            if e == "pe" and key == "pe" and not force_self:
                continue
            if self.waited.get((e, key), 0) < val:
                self.E[e].wait_ge(sem, val)
                self.waited[(e, key)] = val

    @staticmethod
    def _deps(reads, writes):
        t = []
        for b in reads:
            if b.w is not None:
                t.append(b.w)
        for b in writes:
            if b.w is not None:
                t.append(b.w)
            t.extend(b.r.values())
        return t

    @staticmethod
    def _reg(tk, reads, writes):
        for b in reads:
            b.r[tk[2]] = tk
        for b in writes:
            b.w = tk
            b.r = {}

    def op(self, e, fn, reads=(), writes=(), force_self=False):
        self._wait(e, self._deps(reads, writes), force_self)
        inst = fn(self.E[e])
        self.cnt[e] += 1
        inst.then_inc(self.sem[e], 1)
        self._reg((self.sem[e], self.cnt[e], e), reads, writes)

    def dma(self, q, out_ap, in_ap, reads=(), writes=(), sembuf=None, **kw):
        self._wait(q, self._deps(reads, writes))
        sbf = sembuf if sembuf is not None else writes[0]
        if sbf.dsem is None:
            self.nbuf += 1
            sbf.dsem = self.es.enter_context(self.nc.semaphore("d%d_%s" % (self.nbuf, sbf.name)))
            self.dbufs.append(sbf)
        inst = self.E[q].dma_start(out=out_ap, in_=in_ap, **kw)
        sbf.dcnt += 16
        inst.then_inc(sbf.dsem, 16)
        self._reg((sbf.dsem, sbf.dcnt, "d_" + sbf.name), reads, writes)

    def barrier(self):
        ticks = [(self.sem[e], self.cnt[e], e) for e in ("pe", "act", "dve", "pool") if self.cnt[e] > 0]
        ticks += [(b.dsem, b.dcnt, "d_" + b.name) for b in self.dbufs]
        for e in ("sp", "pool", "act", "dve", "pe"):
            self._wait(e, ticks)

    def finish(self, bufs):
        ticks = []
        for b in bufs:
            if b.w is not None:
                ticks.append(b.w)
            ticks.extend(b.r.values())
        self._wait("sp", ticks)


def _subtiles(n):
    r = []
    o = 0
    while o < n:
        r.append((o, min(128, n - o)))
        o += 128
    return r


def build_tail(P):
    D, F, AW, SW, TOK, TT = P["D"], P["F"], P["AW"], P["SW"], P["TOK"], P["TT"]
    KC, FC, AC, SC = D // 128, F // 128, AW // 128, SW // 128
    FG = P.get("FG", 22)
    EPS = 1e-6
    nc = bass.Bass("TRN2", target_bir_lowering=False)
    es = ExitStack()
    with es:
        cx = Ctx(nc, es)
        xsT = cx.ext("xsT", [D, TOK], F32)
        ATd = cx.ext("AT", [AW, TOK], F32)
        ZTd = cx.ext("ZT", [SW, TOK], F32)
        vecs_d = cx.ext("vecs", [128, 9 * KC], F32)
        w_gate = cx.ext("w_gate", [D, 2 * D], F32)
        w_glu = cx.ext("w_glu", [SW, SW], F32)
        w_a = cx.ext("w_a", [AW, D], F32)
        w_s = cx.ext("w_s", [SW, D], F32)
        w_out = cx.ext("w_out", [D, D], F32)
        w_up = cx.ext("w_up", [D, 2 * F], F32)
        cw_d = cx.ext("cw", [128, 3 * 2 * FC], F32)
        cb_d = cx.ext("cb", [128, 2 * FC], F32)
        w_down = cx.ext("w_down", [F, D], F32)
        hmask_d = cx.ext("hmask", [128, 1], F32)
        ident_d = cx.ext("ident", [128, 128], F32)
        outT = cx.ext("outT", [D, TOK - 2], F32, out=True)

        WS = 8192
        wslot = [cx.sb("wslot%d" % i, [128, WS], BF16) for i in range(2)]
        stage = [cx.sb("stage%d" % i, [128, 1024], F32) for i in range(2)]
        xT = cx.sb("xT", [128, KC, TT], F32)
        hT = cx.sb("hT", [128, KC, TT], BF16)
        r1n = max((AC + SC) * TT, FG * TT)
        R1 = cx.sb("R1", [128, r1n], BF16)
        ZM = cx.sb("ZM", [128, KC, TT], BF16)
        vecs = cx.sb("vecs_sb", [128, 9 * KC], F32)
        G1 = cx.sb("G1", [128, KC], F32)
        G2 = cx.sb("G2", [128, KC], F32)
        cw = cx.sb("cw_sb", [128, 3 * 2 * FC], F32)
        cb = cx.sb("cb_sb", [128, 2 * FC], F32)
        carry = cx.sb("carry", [128, 2 * FC, 2], F32)
        hmask = cx.sb("hmask_sb", [128, 1], F32)
        identf = cx.sb("identf", [128, 128], F32)
        onesb = cx.sb("onesb", [128, 128], BF16)
        epsb = cx.sb("epsb", [128, 1], F32)
        zcol = cx.sb("zcol", [128, 2], F32)
        upb = [cx.sb("upb%d" % i, [128, TT + 2], F32) for i in range(4)]
        tmpf = [cx.sb("tmpf%d" % i, [128, TT], F32) for i in range(6)]
        sqb = [cx.sb("sqb%d" % i, [128, TT], BF16) for i in range(2)]
        rstd = cx.sb("rstd", [128, TT], F32)
        banks = [cx.ps("bank%d" % i, [128, 512], F32) for i in range(8)]
        st = {"bank": 0, "slot": 0, "tmp": 0, "stage": 0, "ti": 0, "wl": 0}

        def bank():
            b = banks[st["bank"] % 8]
            st["bank"] += 1
            return b

        def tmp():
            b = tmpf[st["tmp"] % 6]
            st["tmp"] += 1
            return b

        NLMAX = P.get("NL", 256)
        wscs = [cx.dram("wsc%d" % i, [32, 128, WS], BF16) for i in range((NLMAX + 31) // 32)]

        def wload(pieces):
            s = wslot[st["slot"] % 2]
            st["slot"] += 1
            idx = st["wl"]
            st["wl"] += 1
            assert idx < NLMAX
            views = []
            off = 0
            for (wb, r0, nr, c0, ncl) in pieces:
                kc = nr // 128
                v = s.t[:, off:off + kc * ncl].rearrange("p (k n) -> p k n", n=ncl)
                if st["ti"] == 0:
                    src = wb.t[r0:r0 + nr, c0:c0 + ncl].rearrange("(k p) n -> p k n", p=128)
                    cx.dma("pool", v, src, reads=[wb], writes=[s])
                views.append(v)
                off += kc * ncl
            assert off <= WS
            wsc = wscs[idx // 32]
            if st["ti"] == 0:
                cx.dma("sp", wsc.t[idx % 32, :, 0:off], s.t[:, 0:off], reads=[s], writes=[wsc], sembuf=s)
            else:
                cx.dma("sp", s.t[:, 0:off], wsc.t[idx % 32, :, 0:off], reads=[wsc], writes=[s])
            return s, views

        cx.dma("sp", vecs.t[:, :], vecs_d.t[:, :], reads=[vecs_d], writes=[vecs])
        cx.dma("sp", cw.t[:, :], cw_d.t[:, :], reads=[cw_d], writes=[cw])
        cx.dma("sp", cb.t[:, :], cb_d.t[:, :], reads=[cb_d], writes=[cb])
        cx.dma("sp", hmask.t[:, :], hmask_d.t[:, :], reads=[hmask_d], writes=[hmask])
        cx.dma("sp", identf.t[:, :], ident_d.t[:, :], reads=[ident_d], writes=[identf])
        cx.op("dve", lambda e: e.memset(onesb.t[:, :], 1.0), writes=[onesb])
        cx.op("dve", lambda e: e.memset(epsb.t[:, :], EPS), writes=[epsb])
        cx.op("dve", lambda e: e.memset(zcol.t[:, :], 0.0), writes=[zcol])

        def vcol(s, c):
            return vecs.t[:, s * KC + c:s * KC + c + 1]
        cx.op("dve", lambda e: e.scalar_tensor_tensor(out=G1.t[:, :], in0=vecs.t[:, 1 * KC:2 * KC], scalar=1.0,
                                                     in1=vecs.t[:, 6 * KC:7 * KC], op0=ALU.add, op1=ALU.mult),
              reads=[vecs], writes=[G1])
        cx.op("dve", lambda e: e.scalar_tensor_tensor(out=G2.t[:, :], in0=vecs.t[:, 4 * KC:5 * KC], scalar=1.0,
                                                     in1=vecs.t[:, 7 * KC:8 * KC], op0=ALU.add, op1=ALU.mult),
              reads=[vecs], writes=[G2])

        def fnorm(n, Gap, shift_fn, outbuf):
            ssb = bank()
            for c in range(KC):
                sq = sqb[c % 2]
                cx.op("act", lambda e, c=c, sq=sq: e.activation(out=sq.t[:, :n], in_=xT.t[:, c, :n], func=AF.Square),
                      reads=[xT], writes=[sq])
                cx.op("pe", lambda e, c=c, sq=sq: e.matmul(ssb.t[:, :n], lhsT=onesb.t[:, :], rhs=sq.t[:, :n],
                                                         start=(c == 0), stop=(c == KC - 1)),
                      reads=[onesb, sq], writes=[ssb])
            cx.op("act", lambda e: e.activation(out=rstd.t[:, :n], in_=ssb.t[:, :n], func=AF.Sqrt,
                                                bias=epsb.t[:, 0:1], scale=1.0 / D),
                  reads=[ssb, epsb], writes=[rstd])
            cx.op("dve", lambda e: e.reciprocal(out=rstd.t[:, :n], in_=rstd.t[:, :n]), reads=[rstd], writes=[rstd])
            for c in range(KC):
                t = tmp()
                cx.op("dve", lambda e, c=c, t=t: e.tensor_tensor(out=t.t[:, :n], in0=xT.t[:, c, :n], in1=rstd.t[:, :n],
                                                               op=ALU.mult),
                      reads=[xT, rstd], writes=[t])
                bias = shift_fn(c) if shift_fn is not None else zcol.t[:, 0:1]
                cx.op("act", lambda e, c=c, t=t, bias=bias: e.activation(out=outbuf.t[:, c, :n], in_=t.t[:, :n],
                                                                        func=AF.Identity, bias=bias,
                                                                        scale=Gap[:, c:c + 1]),
                      reads=[t, vecs, G1, G2, zcol], writes=[outbuf])

        ntiles = (TOK + TT - 1) // TT
        for ti in range(ntiles):
            t0 = ti * TT
            n = min(TT, TOK - t0)
            subs = _subtiles(n)
            st["ti"] = ti
            st["wl"] = 0
            for c8 in range(0, KC, 8):
                ce = min(KC, c8 + 8)
                cx.dma("sp", xT.t[:, c8:ce, 0:n], xsT.t[c8 * 128:ce * 128, t0:t0 + n].rearrange("(c p) t -> p c t", p=128),
                       reads=[xsT], writes=[xT])
            fnorm(n, G1.t, lambda c: vcol(0, c), hT)
            ATv = R1.t[:, 0:AC * TT].rearrange("p (c t) -> p c t", t=TT)
            STv = R1.t[:, AC * TT:(AC + SC) * TT].rearrange("p (c t) -> p c t", t=TT)
            cx.dma("pool", ATv[:, :, 0:n], ATd.t[:, t0:t0 + n].rearrange("(c p) t -> p c t", p=128),
                   reads=[ATd], writes=[R1])
            cx.dma("pool", ZM.t[:, 0:SC, 0:n], ZTd.t[:, t0:t0 + n].rearrange("(c p) t -> p c t", p=128),
                   reads=[ZTd], writes=[ZM])
            gcols = min(SW, WS // SC // 128 * 128)
            for g0 in range(0, SW, gcols):
                s, (wv,) = wload([(w_glu, 0, SW, g0, gcols)])
                for j in range(gcols // 128):
                    nch = g0 // 128 + j
                    pb = bank()
                    for k in range(SC):
                        cx.op("pe", lambda e, k=k, j=j, pb=pb, wv=wv: e.matmul(
                            pb.t[:, :n], lhsT=wv[:, k, j * 128:(j + 1) * 128], rhs=ZM.t[:, k, :n],
                            start=(k == 0), stop=(k == SC - 1)), reads=[s, ZM], writes=[pb])
                    t = tmp()
                    cx.op("act", lambda e, pb=pb, t=t: e.activation(out=t.t[:, :n], in_=pb.t[:, :n], func=AF.Sigmoid),
                          reads=[pb], writes=[t])
                    cx.op("dve", lambda e, t=t, nch=nch: e.tensor_tensor(out=STv[:, nch, :n], in0=ZM.t[:, nch, :n],
                                                                       in1=t.t[:, :n], op=ALU.mult),
                          reads=[ZM, t], writes=[R1])
            for nch in range(KC):
                s1, (wgg,) = wload([(w_gate, 0, D, nch * 256, 256)])
                wga, wgs = wgg[:, :, 0:128], wgg[:, :, 128:256]
                s2, (wa, wsb) = wload([(w_a, 0, AW, nch * 128, 128), (w_s, 0, SW, nch * 128, 128)])
                pga, pgs, pa, pss = bank(), bank(), bank(), bank()
                for (pb, wv, s, src, kcn) in ((pga, wga, s1, hT, KC), (pgs, wgs, s1, hT, KC)):
                    for k in range(kcn):
                        cx.op("pe", lambda e, k=k, pb=pb, wv=wv, src=src, kcn=kcn: e.matmul(
                            pb.t[:, :n], lhsT=wv[:, k, :], rhs=src.t[:, k, :n], start=(k == 0), stop=(k == kcn - 1)),
                            reads=[s, src], writes=[pb])
                for (pb, wv, srcv, kcn) in ((pa, wa, ATv, AC), (pss, wsb, STv, SC)):
                    for k in range(kcn):
                        cx.op("pe", lambda e, k=k, pb=pb, wv=wv, srcv=srcv, kcn=kcn: e.matmul(
                            pb.t[:, :n], lhsT=wv[:, k, :], rhs=srcv[:, k, :n], start=(k == 0), stop=(k == kcn - 1)),
                            reads=[s2, R1], writes=[pb])
                ta, tb, tc, td = tmp(), tmp(), tmp(), tmp()
                cx.op("act", lambda e: e.activation(out=ta.t[:, :n], in_=pga.t[:, :n], func=AF.Sigmoid),
                      reads=[pga], writes=[ta])
                cx.op("act", lambda e: e.activation(out=tb.t[:, :n], in_=pgs.t[:, :n], func=AF.Sigmoid),
                      reads=[pgs], writes=[tb])
                cx.op("dve", lambda e: e.tensor_tensor(out=tc.t[:, :n], in0=ta.t[:, :n], in1=pa.t[:, :n], op=ALU.mult),
                      reads=[ta, pa], writes=[tc])
                cx.op("dve", lambda e: e.tensor_tensor(out=td.t[:, :n], in0=tb.t[:, :n], in1=pss.t[:, :n], op=ALU.mult),
                      reads=[tb, pss], writes=[td])
                cx.op("dve", lambda e: e.tensor_tensor(out=ZM.t[:, nch, :n], in0=tc.t[:, :n], in1=td.t[:, :n],
                                                       op=ALU.add), reads=[tc, td], writes=[ZM])
            for n0 in range(0, KC, 2):
                ncn = min(2, KC - n0)
                s, (wv,) = wload([(w_out, 0, D, n0 * 128, ncn * 128)])
                for j in range(ncn):
                    nch = n0 + j
                    pb = bank()
                    for k in range(KC):
                        cx.op("pe", lambda e, k=k, j=j, pb=pb, wv=wv: e.matmul(
                            pb.t[:, :n], lhsT=wv[:, k, j * 128:(j + 1) * 128], rhs=ZM.t[:, k, :n],
                            start=(k == 0), stop=(k == KC - 1)), reads=[s, ZM], writes=[pb])
                    cx.op("dve", lambda e, pb=pb, nch=nch: e.scalar_tensor_tensor(
                        out=xT.t[:, nch, :n], in0=pb.t[:, :n], scalar=vcol(2, nch), in1=xT.t[:, nch, :n],
                        op0=ALU.mult, op1=ALU.add), reads=[pb, vecs, xT], writes=[xT])
            fnorm(n, G2.t, lambda c: vcol(3, c), hT)
            ACTv = R1.t[:, 0:FG * TT].rearrange("p (c t) -> p c t", t=TT)
            for f0 in range(0, FC, FG):
                fgn = min(FG, FC - f0)
                for fi in range(fgn):
                    fc = f0 + fi
                    s, (wuu,) = wload([(w_up, 0, D, fc * 256, 256)])
                    wua, wug = wuu[:, :, 0:128], wuu[:, :, 128:256]
                    res = []
                    for kind, wv in ((0, wua), (1, wug)):
                        ch = kind * FC + fc
                        pb = bank()
                        for k in range(KC):
                            cx.op("pe", lambda e, k=k, pb=pb, wv=wv: e.matmul(
                                pb.t[:, :n], lhsT=wv[:, k, :], rhs=hT.t[:, k, :n], start=(k == 0), stop=(k == KC - 1)),
                                reads=[s, hT], writes=[pb])
                        ub = upb[(2 * fc + kind) % 4]
                        cx.op("act", lambda e, pb=pb, ub=ub: e.activation(out=ub.t[:, 2:2 + n], in_=pb.t[:, :n],
                                                                          func=AF.Identity), reads=[pb], writes=[ub])
                        if ti == 0:
                            cx.op("dve", lambda e, ub=ub: e.tensor_scalar(out=ub.t[:, 2:4], in0=ub.t[:, 2:4],
                                                                          scalar1=hmask.t[:, 0:1], scalar2=None,
                                                                          op0=ALU.mult), reads=[ub, hmask], writes=[ub])
                            cx.op("dve", lambda e, ub=ub: e.tensor_copy(out=ub.t[:, 0:2], in_=zcol.t[:, 0:2]),
                                  reads=[zcol], writes=[ub])
                        else:
                            cx.op("dve", lambda e, ub=ub, ch=ch: e.tensor_copy(out=ub.t[:, 0:2], in_=carry.t[:, ch, :]),
                                  reads=[carry], writes=[ub])
                        cx.op("dve", lambda e, ub=ub, ch=ch: e.tensor_copy(out=carry.t[:, ch, :], in_=ub.t[:, n:n + 2]),
                              reads=[ub], writes=[carry])
                        c0, c1, c2 = tmp(), tmp(), tmp()
                        cx.op("act", lambda e, ub=ub, ch=ch, c0=c0: e.activation(
                            out=c0.t[:, :n], in_=ub.t[:, 2:2 + n], func=AF.Identity, bias=cb.t[:, ch:ch + 1],
                            scale=cw.t[:, 2 * 2 * FC + ch:2 * 2 * FC + ch + 1]), reads=[ub, cb, cw], writes=[c0])
                        cx.op("dve", lambda e, ub=ub, ch=ch, c0=c0, c1=c1: e.scalar_tensor_tensor(
                            out=c1.t[:, :n], in0=ub.t[:, 1:1 + n], scalar=cw.t[:, 2 * FC + ch:2 * FC + ch + 1],
                            in1=c0.t[:, :n], op0=ALU.mult, op1=ALU.add), reads=[ub, cw, c0], writes=[c1])
                        cx.op("dve", lambda e, ub=ub, ch=ch, c1=c1, c2=c2: e.scalar_tensor_tensor(
                            out=c2.t[:, :n], in0=ub.t[:, 0:n], scalar=cw.t[:, ch:ch + 1],
                            in1=c1.t[:, :n], op0=ALU.mult, op1=ALU.add), reads=[ub, cw, c1], writes=[c2])
                        res.append(c2)
                    ca, cg = res
                    sg_ = tmp()
                    cx.op("act", lambda e, cg=cg, sg_=sg_: e.activation(out=sg_.t[:, :n], in_=cg.t[:, :n], func=AF.Silu),
                          reads=[cg], writes=[sg_])
                    cx.op("dve", lambda e, ca=ca, sg_=sg_, fi=fi: e.tensor_tensor(
                        out=ACTv[:, fi, :n], in0=sg_.t[:, :n], in1=ca.t[:, :n], op=ALU.mult),
                        reads=[sg_, ca], writes=[R1])
                for n0 in range(0, KC, 2):
                    ncn = min(2, KC - n0)
                    s, (wv,) = wload([(w_down, f0 * 128, fgn * 128, n0 * 128, ncn * 128)])
                    for j in range(ncn):
                        nch = n0 + j
                        pb = bank()
                        for k in range(fgn):
                            cx.op("pe", lambda e, k=k, j=j, pb=pb, wv=wv: e.matmul(
                                pb.t[:, :n], lhsT=wv[:, k, j * 128:(j + 1) * 128], rhs=ACTv[:, k, :n],
                                start=(k == 0), stop=(k == fgn - 1)), reads=[s, R1], writes=[pb])
                        cx.op("dve", lambda e, pb=pb, nch=nch: e.scalar_tensor_tensor(
                            out=xT.t[:, nch, :n], in0=pb.t[:, :n], scalar=vcol(5, nch), in1=xT.t[:, nch, :n],
                            op0=ALU.mult, op1=ALU.add), reads=[pb, vecs, xT], writes=[xT])
            fnorm(n, vecs.t[:, 8 * KC:9 * KC], None, xT)
            p0 = 2 if ti == 0 else 0
            for c8 in range(0, KC, 8):
                ce = min(KC, c8 + 8)
                cx.dma("sp", outT.t[c8 * 128:ce * 128, t0 + p0 - 2:t0 + n - 2].rearrange("(c p) t -> p c t", p=128),
                       xT.t[:, c8:ce, p0:n], reads=[xT], writes=[outT], sembuf=xT)
        cx.barrier()
    return nc


def build_mixer(P):
    D, L, NH, NG = P["D"], P["L"], P["NH"], P["NG"]
    LAM_INIT = 0.2
    KC = D // 128
    TT = min(512, L)
    NQB = L // 128
    HW = NH * 128
    UW = NG * 16
    UC = UW // 128
    NP = NG // 2
    EPS = 1e-6
    PI = float(np.pi)
    nc = bass.Bass("TRN2", target_bir_lowering=False)
    es = ExitStack()
    with es:
        cx = Ctx(nc, es)
        xbT = cx.ext("xbT", [D, L], F32)
        cT_d = cx.ext("cT", [128, KC], F32)
        ada_w = cx.ext("ada_w", [D, 6 * D], F32)
        ada_bT = cx.ext("ada_bT", [128, 6 * KC], F32)
        n1g_d = cx.ext("n1g", [128, KC], F32)
        w_q = cx.ext("w_q", [D, HW], F32)
        w_k = cx.ext("w_k", [D, HW], F32)
        w_v = cx.ext("w_v", [D, HW], F32)
        w_u = cx.ext("w_u", [D, UW], F32)
        lam4_d = cx.ext("lam4", [128, 256], F32)
        subg_d = cx.ext("subg", [128, 128], F32)
        kbias_d = cx.ext("kbias", [128, NH * NQB], F32)
        cmask_d = cx.ext("cmask", [128, 256], F32)
        ident_d = cx.ext("ident", [128, 128], F32)
        arep_d = cx.ext("arep", [128, 3 * NG * 64], F32)
        apair_d = cx.ext("apair", [128, 3 * NP], F32)
        Bre_d = cx.ext("Bre_big", [128, UC * 512], F32)
        Bim_d = cx.ext("Bim_big", [128, UC * 512], F32)
        Cre_d = cx.ext("Cre_pad", [128, NP * 128], F32)
        Cim_d = cx.ext("Cim_pad", [128, NP * 128], F32)
        dskip_d = cx.ext("dskipT", [128, UC], F32)
        iotap_d = cx.ext("iota_p", [128, 1], F32)
        iotaf_d = cx.ext("iota_f", [128, 128], F32)
        tri_d = cx.ext("tri", [128, 128], F32)
        modT_o = cx.ext("modT", [128, 6 * KC], F32, out=True)
        A_o = cx.ext("A_out", [L, HW], F32, out=True)
        zT_o = cx.ext("zT", [UW, L], F32, out=True)
        dbg_o = cx.ext("dbg", [128, 4096], F32, out=True) if P.get("dbg") else None
        QT = cx.dram("QT", [HW, L], BF16)
        KT = cx.dram("KT", [HW, L], BF16)
        Vd = cx.dram("Vd", [L, HW], BF16)
        UT = cx.dram("UT", [UW, L], F32)

        WS = 8192
        wslot = [cx.sb("wslot%d" % i, [128, WS], BF16) for i in range(2)]
        stage = [cx.sb("stage%d" % i, [128, 1024], F32) for i in range(2)]
        identf = cx.sb("identf", [128, 128], F32)
        onesb = cx.sb("onesb", [128, 128], BF16)
        epsb = cx.sb("epsb", [128, 1], F32)
        zcol = cx.sb("zcol", [128, 2], F32)
        modT = cx.sb("modT_sb", [128, 6 * KC], F32)
        n1g = cx.sb("n1g_sb", [128, KC], F32)
        G1 = cx.sb("G1", [128, KC], F32)
        banks = [cx.ps("bank%d" % i, [128, 512], F32) for i in range(8)]
        st = {"bank": 0, "slot": 0, "stage": 0}

        def bank():
            b = banks[st["bank"] % 8]
            st["bank"] += 1
            return b

        def wload(pieces):
            s = wslot[st["slot"] % 2]
            st["slot"] += 1
            views = []
            off = 0
            for (wb, r0, nr, c0, ncl) in pieces:
                kc = nr // 128
                v = s.t[:, off:off + kc * ncl].rearrange("p (k n) -> p k n", n=ncl)
                src = wb.t[r0:r0 + nr, c0:c0 + ncl].rearrange("(k p) n -> p k n", p=128)
                cx.dma("pool", v, src, reads=[wb], writes=[s])
                views.append(v)
                off += kc * ncl
            assert off <= WS
            return s, views

        cx.dma("sp", identf.t[:, :], ident_d.t[:, :], reads=[ident_d], writes=[identf])
        cx.dma("sp", n1g.t[:, :], n1g_d.t[:, :], reads=[n1g_d], writes=[n1g])
        cx.op("dve", lambda e: e.memset(onesb.t[:, :], 1.0), writes=[onesb])
        cx.op("dve", lambda e: e.memset(epsb.t[:, :], EPS), writes=[epsb])
        cx.op("dve", lambda e: e.memset(zcol.t[:, :], 0.0), writes=[zcol])

        with ExitStack() as es0:
            cxs = lambda name, shape, dt: Buf(es0.enter_context(nc.sbuf_tensor(name + "_s", list(shape), dt)), name)
            cTb = cxs("cTb", [128, KC], BF16)
            rowb = [cxs("rowb%d" % i, [1, 256], F32) for i in range(2)]
            onef = cxs("onef", [1, 1], F32)
            adb = cxs("adb", [128, 6 * KC], F32)
            cx.dma("pool", cTb.t[:, :], cT_d.t[:, :], reads=[cT_d], writes=[cTb])
            cx.dma("sp", adb.t[:, :], ada_bT.t[:, :], reads=[ada_bT], writes=[adb])
            cx.op("dve", lambda e: e.memset(onef.t[:, :], 1.0), writes=[onef])
            mp = bank()
            m0s = [cxs("m0s%d" % i, [128, WS], BF16) for i in range(6)]
            for blk in range(6 * D // 256):
                s = m0s[blk % 6]
                wv = s.t[:, 0:KC * 256].rearrange("p (k n) -> p k n", n=256)
                cx.dma("pool", wv, ada_w.t[:, blk * 256:(blk + 1) * 256].rearrange("(k p) n -> p k n", p=128),
                       reads=[ada_w], writes=[s])
                pb = bank()
                if pb is mp:
                    pb = bank()
                for k in range(KC):
                    cx.op("pe", lambda e, k=k, pb=pb, wv=wv: e.matmul(pb.t[0:1, 0:256], lhsT=cTb.t[:, k:k + 1],
                                                                     rhs=wv[:, k, :], start=(k == 0), stop=(k == KC - 1)),
                          reads=[s, cTb], writes=[pb])
                rb = rowb[blk % 2]
                cx.op("act", lambda e, pb=pb, rb=rb: e.activation(out=rb.t[0:1, :], in_=pb.t[0:1, 0:256], func=AF.Identity),
                      reads=[pb], writes=[rb])
                for j in range(2):
                    col = blk * 2 + j
                    cx.op("pe", lambda e, j=j, col=col, rb=rb: e.matmul(mp.t[:, col:col + 1],
                                                                      lhsT=rb.t[0:1, j * 128:(j + 1) * 128],
                                                                      rhs=onef.t[0:1, 0:1], start=True, stop=True),
                          reads=[rb, onef], writes=[mp])
            cx.op("dve", lambda e: e.tensor_tensor(out=modT.t[:, :], in0=mp.t[:, 0:6 * KC], in1=adb.t[:, :], op=ALU.add),
                  reads=[mp, adb], writes=[modT])
            cx.dma("sp", modT_o.t[:, :], modT.t[:, :], reads=[modT], writes=[modT_o], sembuf=modT)
            cx.op("dve", lambda e: e.scalar_tensor_tensor(out=G1.t[:, :], in0=modT.t[:, KC:2 * KC], scalar=1.0,
                                                         in1=n1g.t[:, :], op0=ALU.add, op1=ALU.mult),
                  reads=[modT, n1g], writes=[G1])
            cx.barrier()

        with ExitStack() as es1:
            cxs = lambda name, shape, dt: Buf(es1.enter_context(nc.sbuf_tensor(name + "_s", list(shape), dt)), name)
            xT = cxs("xT", [128, KC, TT], BF16)
            hT = cxs("hT", [128, KC, TT], BF16)
            sqb = [cxs("sqb%d" % i, [128, TT], BF16) for i in range(2)]
            rstd = cxs("rstd", [128, TT], F32)
            tmpf = [cxs("tmpf%d" % i, [128, TT], F32) for i in range(3)]
            obf = [cxs("obf%d" % i, [128, 512], BF16) for i in range(3)]
            of32 = [cxs("of32%d" % i, [128, 512], F32) for i in range(2)]
            cnt = {"t": 0, "o": 0, "f": 0}
            for ti in range(L // TT):
                t0 = ti * TT
                n = TT
                cx.dma("pool", xT.t[:, :, 0:n], xbT.t[:, t0:t0 + n].rearrange("(c p) t -> p c t", p=128), reads=[xbT],
                       writes=[xT])
                ssb = bank()
                for c in range(KC):
                    sq = sqb[c % 2]
                    cx.op("act", lambda e, c=c, sq=sq: e.activation(out=sq.t[:, :n], in_=xT.t[:, c, :n], func=AF.Square),
                          reads=[xT], writes=[sq])
                    cx.op("pe", lambda e, c=c, sq=sq: e.matmul(ssb.t[:, :n], lhsT=onesb.t[:, :], rhs=sq.t[:, :n],
                                                             start=(c == 0), stop=(c == KC - 1)),
                          reads=[onesb, sq], writes=[ssb])
                cx.op("act", lambda e: e.activation(out=rstd.t[:, :n], in_=ssb.t[:, :n], func=AF.Sqrt,
                                                    bias=epsb.t[:, 0:1], scale=1.0 / D), reads=[ssb, epsb], writes=[rstd])
                cx.op("dve", lambda e: e.reciprocal(out=rstd.t[:, :n], in_=rstd.t[:, :n]), reads=[rstd], writes=[rstd])
                for c in range(KC):
                    t = tmpf[cnt["t"] % 3]
                    cnt["t"] += 1
                    cx.op("dve", lambda e, c=c, t=t: e.tensor_tensor(out=t.t[:, :n], in0=xT.t[:, c, :n], in1=rstd.t[:, :n],
                                                                   op=ALU.mult), reads=[xT, rstd], writes=[t])
                    cx.op("act", lambda e, c=c, t=t: e.activation(out=hT.t[:, c, :n], in_=t.t[:, :n], func=AF.Identity,
                                                                  bias=modT.t[:, c:c + 1], scale=G1.t[:, c:c + 1]),
                          reads=[t, modT, G1], writes=[hT])
                for (wb, dst, width, isf32) in ((w_q, QT, HW, False), (w_k, KT, HW, False), (w_u, UT, UW, True)):
                    for n0 in range(0, width, 256):
                        ncl = min(256, width - n0)
                        s, (wv,) = wload([(wb, 0, D, n0, ncl)])
                        for j in range(ncl // 128):
                            pb = bank()
                            for k in range(KC):
                                cx.op("pe", lambda e, k=k, j=j, pb=pb, wv=wv: e.matmul(
                                    pb.t[:, :n], lhsT=wv[:, k, j * 128:(j + 1) * 128], rhs=hT.t[:, k, :n],
                                    start=(k == 0), stop=(k == KC - 1)), reads=[s, hT], writes=[pb])
                            if isf32:
                                ob = of32[cnt["f"] % 2]
                                cnt["f"] += 1
                            else:
                                ob = obf[cnt["o"] % 3]
                                cnt["o"] += 1
                            cx.op("act", lambda e, pb=pb, ob=ob: e.activation(out=ob.t[:, :n], in_=pb.t[:, :n],
                                                                              func=AF.Identity), reads=[pb], writes=[ob])
                            r0 = n0 + j * 128
                            cx.dma("sp", dst.t[r0:r0 + 128, t0:t0 + n], ob.t[:, :n], reads=[ob], writes=[dst], sembuf=ob)
                for n0 in range(0, HW, 256):
                    ncl = min(256, HW - n0)
                    s, (wv,) = wload([(w_v, 0, D, n0, ncl)])
                    for (so, sn) in _subtiles(n):
                        pb = bank()
                        for k in range(KC):
                            cx.op("pe", lambda e, k=k, pb=pb, wv=wv, so=so, sn=sn: e.matmul(
                                pb.t[0:sn, 0:ncl], lhsT=hT.t[:, k, so:so + sn], rhs=wv[:, k, :],
                                start=(k == 0), stop=(k == KC - 1)), reads=[s, hT], writes=[pb])
                        ob = obf[cnt["o"] % 3]
                        cnt["o"] += 1
                        cx.op("act", lambda e, pb=pb, ob=ob, sn=sn: e.activation(out=ob.t[0:sn, 0:ncl], in_=pb.t[0:sn, 0:ncl],
                                                                                func=AF.Identity), reads=[pb], writes=[ob])
                        cx.dma("sp", Vd.t[t0 + so:t0 + so + sn, n0:n0 + ncl], ob.t[0:sn, 0:ncl], reads=[ob], writes=[Vd],
                               sembuf=ob)
            allb = [xT, hT, rstd] + sqb + tmpf + obf + of32
            cx.barrier()

        if P.get("attn", True):
            _mixer_attention(cx, nc, P, dict(QT=QT, KT=KT, Vd=Vd, A_o=A_o, lam4_d=lam4_d, subg_d=subg_d,
                                             kbias_d=kbias_d, cmask_d=cmask_d, banks=banks, epsb=epsb))
        if P.get("s5", True):
            _mixer_s5(cx, nc, P, dict(UT=UT, zT_o=zT_o, arep_d=arep_d, apair_d=apair_d, Bre_d=Bre_d, Bim_d=Bim_d,
                                      Cre_d=Cre_d, Cim_d=Cim_d, dskip_d=dskip_d, iotap_d=iotap_d, iotaf_d=iotaf_d,
                                      tri_d=tri_d, banks=banks, dbg=dbg_o))
        cx.barrier()
    return nc


def _mixer_attention(cx, nc, P, T):
    L, NH = P["L"], P["NH"]
    LAM_INIT = 0.2
    NQB = L // 128
    SCALE = 64 ** -0.5
    QT, KT, Vd, A_o, banks, epsb = T["QT"], T["KT"], T["Vd"], T["A_o"], T["banks"], T["epsb"]
    with ExitStack() as es2:
        cxs = lambda name, shape, dt: Buf(es2.enter_context(nc.sbuf_tensor(name + "_s", list(shape), dt)), name)
        QTh = [cxs("QTh%d" % i, [128, L], BF16) for i in range(2)]
        KTh = [cxs("KTh%d" % i, [128, L], BF16) for i in range(2)]
        Vh = [cxs("Vh%d" % i, [128, NQB, 129], BF16) for i in range(2)]
        lam4 = cxs("lam4", [128, 256], F32)
        subg = cxs("subg", [128, 128], F32)
        kbias = cxs("kbias", [128, NH * NQB], F32)
        cmask = cxs("cmask", [128, 256], F32)
        pT0 = [cxs("pT0_%d" % i, [128, 256], BF16) for i in range(3)]
        pT1 = [cxs("pT1_%d" % i, [128, 256], BF16) for i in range(3)]
        pF0 = [cxs("pF0_%d" % i, [128, 256], F32) for i in range(2)]
        pF1 = [cxs("pF1_%d" % i, [128, 256], F32) for i in range(2)]
        sm = cxs("sm", [128, 16], F32)
        ltmp = cxs("ltmp", [128, 128], F32)
        of = [cxs("of%d" % i, [128, 128], F32) for i in range(2)]
        sqt = cxs("sqt", [128, 128], F32)
        rr = [cxs("rr%d" % i, [128, 8], F32) for i in range(2)]
        Ao = [cxs("Ao%d" % i, [128, 128], F32) for i in range(2)]
        for (sbt, d) in ((lam4, T["lam4_d"]), (subg, T["subg_d"]), (kbias, T["kbias_d"]), (cmask, T["cmask_d"])):
            cx.dma("sp", sbt.t[:, :], d.t[:, :], reads=[d], writes=[sbt])
        for i in range(2):
            cx.op("dve", lambda e, i=i: e.tensor_tensor(out=ltmp.t[:, 0:64], in0=lam4.t[:, i * 128:i * 128 + 64],
                                                       in1=lam4.t[:, i * 128 + 64:i * 128 + 128], op=ALU.mult),
                  reads=[lam4], writes=[ltmp])
            cx.op("dve", lambda e, i=i: e.reduce_sum(out=sm.t[:, i:i + 1], in_=ltmp.t[:, 0:64], axis=mybir.AxisListType.X),
                  reads=[ltmp], writes=[sm])
        cx.op("act", lambda e: e.activation(out=sm.t[:, 2:4], in_=sm.t[:, 0:2], func=AF.Exp), reads=[sm], writes=[sm])
        cx.op("dve", lambda e: e.tensor_tensor(out=sm.t[:, 4:5], in0=sm.t[:, 3:4], in1=sm.t[:, 2:3], op=ALU.subtract),
              reads=[sm], writes=[sm])
        cx.op("dve", lambda e: e.tensor_scalar(out=sm.t[:, 4:5], in0=sm.t[:, 4:5], scalar1=-LAM_INIT, scalar2=None,
                                               op0=ALU.add), reads=[sm], writes=[sm])
        cx.op("dve", lambda e: e.tensor_scalar(out=subg.t[:, :], in0=subg.t[:, :], scalar1=1.0 - LAM_INIT, scalar2=None,
                                               op0=ALU.mult), reads=[subg], writes=[subg])
        obanks = banks[0:4]
        pbanks = banks[4:8]
        ctr = {"p": 0, "t": 0, "f": 0, "q": 0}
        for h in range(NH):
            q_, k_, v_ = QTh[h % 2], KTh[h % 2], Vh[h % 2]
            cx.dma("sp", q_.t[:, :], QT.t[h * 128:(h + 1) * 128, :], reads=[QT], writes=[q_])
            cx.dma("sp", k_.t[:, :], KT.t[h * 128:(h + 1) * 128, :], reads=[KT], writes=[k_])
            cx.dma("sp", v_.t[:, :, 0:128], Vd.t[:, h * 128:(h + 1) * 128].rearrange("(b p) e -> p b e", p=128),
                   reads=[Vd], writes=[v_])
            cx.op("dve", lambda e, v_=v_: e.memset(v_.t[:, :, 128:129], 1.0), writes=[v_])
            for m in range(NQB // 2):
                qa, qb_ = 2 * m, 2 * m + 1
                OA1, OA2, OB1, OB2 = obanks

                def emit_scores(step):
                    pss = [pbanks[(ctr["p"] + i) % 4] for i in range(2)]
                    ctr["p"] += 2
                    items = []
                    if step >= 0:
                        items.append((qa, step, 0))
                    items.append((qb_, step + 1, 128))
                    for (qq, kk, col) in items:
                        for hf in range(2):
                            cx.op("pe", lambda e, hf=hf, qq=qq, kk=kk, col=col: e.matmul(
                                pss[hf].t[:, col:col + 128], lhsT=k_.t[hf * 64:(hf + 1) * 64, kk * 128:(kk + 1) * 128],
                                rhs=q_.t[hf * 64:(hf + 1) * 64, qq * 128:(qq + 1) * 128], start=True, stop=True),
                                reads=[k_, q_], writes=[pss[hf]])
                    return pss

                steps = list(range(-1, qa + 1))
                nxt = emit_scores(steps[0])
                for si, step in enumerate(steps):
                    pss = nxt
                    if si + 1 < len(steps):
                        nxt = emit_scores(steps[si + 1])
                    c0 = 128 if step < 0 else 0
                    diag = (step == qa)
                    pts = [pT0[ctr["t"] % 3], pT1[ctr["t"] % 3]]
                    ctr["t"] += 1
                    dlt = qb_ if step < 0 else qa - step
                    bcol = h * NQB + dlt
                    if diag:
                        dsts = [pF0[ctr["f"] % 2], pF1[ctr["f"] % 2]]
                        ctr["f"] += 1
                    else:
                        dsts = pts
                    for hf in range(2):
                        cx.op("act", lambda e, hf=hf: e.activation(
                            out=dsts[hf].t[:, c0:256], in_=pss[hf].t[:, c0:256], func=AF.Exp,
                            bias=kbias.t[:, bcol:bcol + 1], scale=SCALE), reads=[pss[hf], kbias], writes=[dsts[hf]])
                        if diag:
                            cx.op("dve", lambda e, hf=hf: e.tensor_tensor(out=pts[hf].t[:, :], in0=dsts[hf].t[:, :],
                                                                         in1=cmask.t[:, :], op=ALU.mult),
                                  reads=[dsts[hf], cmask], writes=[pts[hf]])
                    pv = []
                    if step >= 0:
                        pv += [(0, OA1, 0, step, step == 0, step == qa), (1, OA2, 0, step, step == 0, step == qa)]
                    pv += [(0, OB1, 128, step + 1, step < 0, step == qa), (1, OB2, 128, step + 1, step < 0, step == qa)]
                    for (hf, Ob, col, kk, st_, sp_) in pv:
                        cx.op("pe", lambda e, hf=hf, Ob=Ob, col=col, kk=kk, st_=st_, sp_=sp_: e.matmul(
                            Ob.t[:, 0:129], lhsT=pts[hf].t[:, col:col + 128], rhs=v_.t[:, kk, :],
                            start=st_, stop=sp_), reads=[pts[hf], v_], writes=[Ob])
                for (qq, O1, O2) in ((qa, OA1, OA2), (qb_, OB1, OB2)):
                    r = rr[qq % 2]
                    o = of[qq % 2]
                    ao = Ao[qq % 2]
                    cx.op("dve", lambda e: e.reciprocal(out=r.t[:, 0:1], in_=O1.t[:, 128:129]), reads=[O1], writes=[r])
                    cx.op("dve", lambda e: e.reciprocal(out=r.t[:, 1:2], in_=O2.t[:, 128:129]), reads=[O2], writes=[r])
                    cx.op("dve", lambda e: e.tensor_tensor(out=r.t[:, 2:3], in0=r.t[:, 1:2], in1=sm.t[:, 4:5], op=ALU.mult),
                          reads=[r, sm], writes=[r])
                    cx.op("dve", lambda e: e.tensor_scalar(out=o.t[:, :], in0=O1.t[:, 0:128], scalar1=r.t[:, 0:1],
                                                           scalar2=None, op0=ALU.mult), reads=[O1, r], writes=[o])
                    cx.op("dve", lambda e: e.scalar_tensor_tensor(out=o.t[:, :], in0=O2.t[:, 0:128], scalar=r.t[:, 2:3],
                                                                  in1=o.t[:, :], op0=ALU.mult, op1=ALU.add),
                          reads=[O2, r, o], writes=[o])
                    cx.op("dve", lambda e: e.tensor_tensor(out=sqt.t[:, :], in0=o.t[:, :], in1=o.t[:, :], op=ALU.mult),
                          reads=[o], writes=[sqt])
                    cx.op("dve", lambda e: e.reduce_sum(out=r.t[:, 3:4], in_=sqt.t[:, :], axis=mybir.AxisListType.X),
                          reads=[sqt], writes=[r])
                    cx.op("act", lambda e: e.activation(out=r.t[:, 4:5], in_=r.t[:, 3:4], func=AF.Sqrt,
                                                        bias=epsb.t[:, 0:1], scale=1.0 / 128), reads=[r, epsb], writes=[r])
                    cx.op("dve", lambda e: e.reciprocal(out=r.t[:, 5:6], in_=r.t[:, 4:5]), reads=[r], writes=[r])
                    cx.op("dve", lambda e: e.scalar_tensor_tensor(
                        out=ao.t[:, :], in0=o.t[:, :], scalar=r.t[:, 5:6], in1=subg.t[:, :], op0=ALU.mult, op1=ALU.mult),
                        reads=[o, r, subg], writes=[ao])
                    cx.dma("sp", A_o.t[qq * 128:(qq + 1) * 128, h * 128:(h + 1) * 128], ao.t[:, :], reads=[ao], writes=[A_o],
                           sembuf=ao)
        cx.barrier()


def _mixer_s5(cx, nc, P, T):
    L, NG = P["L"], P["NG"]
    UC, NP_, GP = NG // 8, NG // 2, NG * 64
    NB = L // 128
    PI = float(np.pi)
    X = mybir.AxisListType.X
    UT, zT_o, banks = T["UT"], T["zT_o"], T["banks"]
    with ExitStack() as es3:
        cxs = lambda name, shape, dt: Buf(es3.enter_context(nc.sbuf_tensor(name + "_s", list(shape), dt)), name)
        Emre = cxs("Emre", [128, GP], F32)
        Emim = cxs("Emim", [128, GP], F32)
        Epre = cxs("Epre", [128, NP_, 128], F32)
        Epim = cxs("Epim", [128, NP_, 128], F32)
        Bbre = cxs("Bbre", [128, GP], BF16)
        Bbim = cxs("Bbim", [128, GP], BF16)
        Cre = cxs("Cre", [128, NP_ * 128], F32)
        Cimn = cxs("Cimn", [128, NP_ * 128], F32)
        trib = cxs("trib", [128, 128], BF16)
        iotap = cxs("iotap", [128, 2], F32)
        iotaf = cxs("iotaf", [128, 128], F32)
        dskip = cxs("dskip", [128, UC], F32)
        abp = cxs("abp", [128, 2, NP_], F32)
        csb = [cxs("cs%d" % i, [128, 2, NP_], F32) for i in range(2)]
        pp = cxs("pp", [128, 12, NP_], F32)
        cx.dma("pool", trib.t[:, :], T["tri_d"].t[:, :], reads=[T["tri_d"]], writes=[trib])
        cx.dma("sp", iotap.t[:, 0:1], T["iotap_d"].t[:, :], reads=[T["iotap_d"]], writes=[iotap])
        cx.dma("sp", iotaf.t[:, :], T["iotaf_d"].t[:, :], reads=[T["iotaf_d"]], writes=[iotaf])
        cx.dma("sp", dskip.t[:, :], T["dskip_d"].t[:, :], reads=[T["dskip_d"]], writes=[dskip])
        cx.dma("sp", Cre.t[:, :], T["Cre_d"].t[:, :], reads=[T["Cre_d"]], writes=[Cre])
        cx.dma("sp", Cimn.t[:, :], T["Cim_d"].t[:, :], reads=[T["Cim_d"]], writes=[Cimn])
        cx.op("dve", lambda e: e.tensor_scalar(out=Cimn.t[:, :], in0=Cimn.t[:, :], scalar1=-1.0, scalar2=None, op0=ALU.mult),
              reads=[Cimn], writes=[Cimn])
        cx.op("dve", lambda e: e.tensor_scalar(out=iotap.t[:, 1:2], in0=iotap.t[:, 0:1], scalar1=-1.0, scalar2=None,
                                               op0=ALU.mult), reads=[iotap], writes=[iotap])

        I32 = mybir.dt.int32

        def sincos(out_sin, out_cos, arg_ap, rbufs, wb, tA, tF, tI, tbufs):
            for (dst, shift) in ((out_sin, 0.0), (out_cos, 0.5 * PI)):
                src = arg_ap
                if shift != 0.0:
                    cx.op("dve", lambda e: e.tensor_scalar(out=tA, in0=arg_ap, scalar1=shift, scalar2=None, op0=ALU.add),
                          reads=rbufs, writes=tbufs)
                    src = tA
                cx.op("dve", lambda e: e.tensor_scalar(out=tI, in0=src, scalar1=1.0 / (2 * PI), scalar2=None, op0=ALU.mult),
                      reads=rbufs + tbufs, writes=tbufs)
                cx.op("dve", lambda e: e.tensor_copy(out=tF, in_=tI), reads=tbufs, writes=tbufs)
                cx.op("dve", lambda e: e.scalar_tensor_tensor(out=tF, in0=tF, scalar=-2 * PI, in1=src, op0=ALU.mult,
                                                             op1=ALU.add), reads=rbufs + tbufs, writes=tbufs)
                cx.op("dve", lambda e: e.tensor_scalar(out=tF, in0=tF, scalar1=-PI, scalar2=PI, op0=ALU.max, op1=ALU.min),
                      reads=tbufs, writes=tbufs)
                cx.op("act", lambda e: e.activation(out=dst, in_=tF, func=AF.Sin), reads=tbufs, writes=[wb])

        with ExitStack() as es4:
            cxt = lambda name, shape, dt: Buf(es4.enter_context(nc.sbuf_tensor(name + "_s", list(shape), dt)), name)
            arep = cxt("arep", [128, 3, GP], F32)
            HC = min(GP, 1024)
            wk = cxt("wk", [128, 9, HC], F32)
            wki = cxt("wki", [128, HC], I32)
            Braw = cxt("Braw", [128, 2, GP], F32)
            cx.dma("sp", arep.t[:, :, :], T["arep_d"].t[:, :].rearrange("p (s n) -> p s n", s=3), reads=[T["arep_d"]],
                   writes=[arep])
            cx.dma("sp", Braw.t[:, 0, :], T["Bre_d"].t[:, :], reads=[T["Bre_d"]], writes=[Braw])
            cx.dma("sp", Braw.t[:, 1, :], T["Bim_d"].t[:, :], reads=[T["Bim_d"]], writes=[Braw])
            for hf_ in range(GP // HC):
                cs = slice(hf_ * HC, (hf_ + 1) * HC)
                cx.op("act", lambda e: e.activation(out=wk.t[:, 0, :], in_=arep.t[:, 2, cs], func=AF.Exp), reads=[arep], writes=[wk])
                cx.op("dve", lambda e: e.tensor_tensor(out=wk.t[:, 1, :], in0=arep.t[:, 0, cs], in1=wk.t[:, 0, :], op=ALU.mult),
                      reads=[arep, wk], writes=[wk])
                cx.op("dve", lambda e: e.tensor_tensor(out=wk.t[:, 2, :], in0=arep.t[:, 1, cs], in1=wk.t[:, 0, :], op=ALU.mult),
                      reads=[arep, wk], writes=[wk])
                cx.op("act", lambda e: e.activation(out=wk.t[:, 3, :], in_=wk.t[:, 1, :], func=AF.Exp, scale=iotap.t[:, 1:2]),
                      reads=[wk, iotap], writes=[wk])
                cx.op("dve", lambda e: e.tensor_scalar(out=wk.t[:, 4, :], in0=wk.t[:, 2, :], scalar1=iotap.t[:, 0:1],
                                                       scalar2=None, op0=ALU.mult), reads=[wk, iotap], writes=[wk])
                sincos(wk.t[:, 5, :], wk.t[:, 6, :], wk.t[:, 4, :], [wk], wk, wk.t[:, 7, :], wk.t[:, 8, :], wki.t[:, :], [wk, wki])
                cx.op("dve", lambda e: e.tensor_tensor(out=Emre.t[:, cs], in0=wk.t[:, 3, :], in1=wk.t[:, 6, :], op=ALU.mult),
                      reads=[wk], writes=[Emre])
                cx.op("dve", lambda e: e.scalar_tensor_tensor(out=Emim.t[:, cs], in0=wk.t[:, 3, :], scalar=-1.0, in1=wk.t[:, 5, :],
                                                             op0=ALU.mult, op1=ALU.mult), reads=[wk], writes=[Emim])
                sincos(wk.t[:, 5, :], wk.t[:, 6, :], wk.t[:, 2, :], [wk], wk, wk.t[:, 7, :], wk.t[:, 8, :], wki.t[:, :], [wk, wki])
                cx.op("act", lambda e: e.activation(out=wk.t[:, 0, :], in_=wk.t[:, 1, :], func=AF.Exp), reads=[wk], writes=[wk])
                cx.op("dve", lambda e: e.tensor_tensor(out=wk.t[:, 3, :], in0=wk.t[:, 0, :], in1=wk.t[:, 6, :], op=ALU.mult),
                      reads=[wk], writes=[wk])
                cx.op("dve", lambda e: e.tensor_tensor(out=wk.t[:, 4, :], in0=wk.t[:, 0, :], in1=wk.t[:, 5, :], op=ALU.mult),
                      reads=[wk], writes=[wk])
                cx.op("dve", lambda e: e.tensor_scalar(out=wk.t[:, 3, :], in0=wk.t[:, 3, :], scalar1=-1.0, scalar2=None,
                                                       op0=ALU.add), reads=[wk], writes=[wk])
                cx.op("dve", lambda e: e.tensor_tensor(out=wk.t[:, 0, :], in0=arep.t[:, 0, cs], in1=arep.t[:, 0, cs], op=ALU.mult),
                      reads=[arep], writes=[wk])
                cx.op("dve", lambda e: e.tensor_tensor(out=wk.t[:, 1, :], in0=arep.t[:, 1, cs], in1=arep.t[:, 1, cs], op=ALU.mult),
                      reads=[arep], writes=[wk])
                cx.op("dve", lambda e: e.tensor_tensor(out=wk.t[:, 0, :], in0=wk.t[:, 0, :], in1=wk.t[:, 1, :], op=ALU.add),
                      reads=[wk], writes=[wk])
                cx.op("dve", lambda e: e.reciprocal(out=wk.t[:, 0, :], in_=wk.t[:, 0, :]), reads=[wk], writes=[wk])
                cx.op("dve", lambda e: e.tensor_tensor(out=wk.t[:, 1, :], in0=wk.t[:, 3, :], in1=arep.t[:, 0, cs], op=ALU.mult),
                      reads=[wk, arep], writes=[wk])
                cx.op("dve", lambda e: e.tensor_tensor(out=wk.t[:, 2, :], in0=wk.t[:, 4, :], in1=arep.t[:, 1, cs], op=ALU.mult),
                      reads=[wk, arep], writes=[wk])
                cx.op("dve", lambda e: e.tensor_tensor(out=wk.t[:, 1, :], in0=wk.t[:, 1, :], in1=wk.t[:, 2, :], op=ALU.add),
                      reads=[wk], writes=[wk])
                cx.op("dve", lambda e: e.tensor_tensor(out=wk.t[:, 5, :], in0=wk.t[:, 1, :], in1=wk.t[:, 0, :], op=ALU.mult),
                      reads=[wk], writes=[wk])
                cx.op("dve", lambda e: e.tensor_tensor(out=wk.t[:, 1, :], in0=wk.t[:, 4, :], in1=arep.t[:, 0, cs], op=ALU.mult),
                      reads=[wk, arep], writes=[wk])
                cx.op("dve", lambda e: e.tensor_tensor(out=wk.t[:, 2, :], in0=wk.t[:, 3, :], in1=arep.t[:, 1, cs], op=ALU.mult),
                      reads=[wk, arep], writes=[wk])
                cx.op("dve", lambda e: e.tensor_tensor(out=wk.t[:, 1, :], in0=wk.t[:, 1, :], in1=wk.t[:, 2, :], op=ALU.subtract),
                      reads=[wk], writes=[wk])
                cx.op("dve", lambda e: e.tensor_tensor(out=wk.t[:, 6, :], in0=wk.t[:, 1, :], in1=wk.t[:, 0, :], op=ALU.mult),
                      reads=[wk], writes=[wk])
                cx.op("dve", lambda e: e.tensor_tensor(out=wk.t[:, 1, :], in0=wk.t[:, 5, :], in1=Braw.t[:, 0, cs], op=ALU.mult),
                      reads=[wk, Braw], writes=[wk])
                cx.op("dve", lambda e: e.tensor_tensor(out=wk.t[:, 2, :], in0=wk.t[:, 6, :], in1=Braw.t[:, 1, cs], op=ALU.mult),
                      reads=[wk, Braw], writes=[wk])
                cx.op("dve", lambda e: e.tensor_tensor(out=Bbre.t[:, cs], in0=wk.t[:, 1, :], in1=wk.t[:, 2, :], op=ALU.subtract),
                      reads=[wk], writes=[Bbre])
                cx.op("dve", lambda e: e.tensor_tensor(out=wk.t[:, 1, :], in0=wk.t[:, 5, :], in1=Braw.t[:, 1, cs], op=ALU.mult),
                      reads=[wk, Braw], writes=[wk])
                cx.op("dve", lambda e: e.tensor_tensor(out=wk.t[:, 2, :], in0=wk.t[:, 6, :], in1=Braw.t[:, 0, cs], op=ALU.mult),
                      reads=[wk, Braw], writes=[wk])
                cx.op("dve", lambda e: e.tensor_tensor(out=Bbim.t[:, cs], in0=wk.t[:, 1, :], in1=wk.t[:, 2, :], op=ALU.add),
                      reads=[wk], writes=[Bbim])
            cx.dma("sp", pp.t[:, 0:3, :], T["apair_d"].t[:, :].rearrange("p (s n) -> p s n", s=3), reads=[T["apair_d"]],
                   writes=[pp])
            cx.op("act", lambda e: e.activation(out=pp.t[:, 5, :], in_=pp.t[:, 2, :], func=AF.Exp), reads=[pp], writes=[pp])
            cx.op("dve", lambda e: e.tensor_tensor(out=pp.t[:, 3, :], in0=pp.t[:, 0, :], in1=pp.t[:, 5, :], op=ALU.mult),
                  reads=[pp], writes=[pp])
            cx.op("dve", lambda e: e.tensor_tensor(out=pp.t[:, 4, :], in0=pp.t[:, 1, :], in1=pp.t[:, 5, :], op=ALU.mult),
                  reads=[pp], writes=[pp])
            ppi = cxt("ppi", [128, NP_], I32)
            sincos(pp.t[:, 6, :], pp.t[:, 7, :], pp.t[:, 4, :], [pp], pp, pp.t[:, 8, :], pp.t[:, 9, :], ppi.t[:, :], [pp, ppi])
            cx.op("act", lambda e: e.activation(out=pp.t[:, 5, :], in_=pp.t[:, 3, :], func=AF.Exp), reads=[pp], writes=[pp])
            cx.op("dve", lambda e: e.tensor_tensor(out=abp.t[:, 0, :], in0=pp.t[:, 5, :], in1=pp.t[:, 7, :], op=ALU.mult),
                  reads=[pp], writes=[abp])
            cx.op("dve", lambda e: e.tensor_tensor(out=abp.t[:, 1, :], in0=pp.t[:, 5, :], in1=pp.t[:, 6, :], op=ALU.mult),
                  reads=[pp], writes=[abp])
            ew = cxt("ew", [128, 6, 128], F32)
            ewi = cxt("ewi", [128, 128], I32)
            for j in range(NP_):
                cx.op("act", lambda e, j=j: e.activation(out=ew.t[:, 0, :], in_=iotaf.t[:, :], func=AF.Exp,
                                                         scale=pp.t[:, 3, j:j + 1]), reads=[iotaf, pp], writes=[ew])
                cx.op("dve", lambda e, j=j: e.tensor_scalar(out=ew.t[:, 1, :], in0=iotaf.t[:, :], scalar1=pp.t[:, 4, j:j + 1],
                                                            scalar2=None, op0=ALU.mult), reads=[iotaf, pp], writes=[ew])
                sincos(ew.t[:, 2, :], ew.t[:, 3, :], ew.t[:, 1, :], [ew], ew, ew.t[:, 4, :], ew.t[:, 5, :], ewi.t[:, :], [ew, ewi])
                cx.op("dve", lambda e, j=j: e.tensor_tensor(out=Epre.t[:, j, :], in0=ew.t[:, 0, :], in1=ew.t[:, 3, :],
                                                           op=ALU.mult), reads=[ew], writes=[Epre])
                cx.op("dve", lambda e, j=j: e.tensor_tensor(out=Epim.t[:, j, :], in0=ew.t[:, 0, :], in1=ew.t[:, 2, :],
                                                           op=ALU.mult), reads=[ew], writes=[Epim])
            cx.barrier()

        if T.get("dbg") is not None:
            dbg = T["dbg"]
            dumps = [(Emre.t[:, 0:512], Emre, 0, 512), (Emim.t[:, 0:512], Emim, 512, 512), (Epre.t[:, 0, :], Epre, 1024, 128),
                     (Epim.t[:, 0, :], Epim, 1152, 128), (abp.t[:, 0, :], abp, 1280, NP_), (abp.t[:, 1, :], abp, 1280 + NP_, NP_),
                     (Bbre.t[:, 0:512], Bbre, 1536, 512), (Bbim.t[:, 0:512], Bbim, 2048, 512)]
            for (ap_, b_, off, n_) in dumps:
                cx.dma("pool", dbg.t[:, off:off + n_], ap_, reads=[b_], writes=[dbg], sembuf=b_)
        TB = min(4, NB)
        NGRP = NB // TB
        uTf = [cxs("uTf%d" % i, [128, UC, TB * 128], F32) for i in range(2)]
        uTb = [cxs("uTb%d" % i, [128, UC, TB * 128], BF16) for i in range(2)]
        Wc = [cxs("Wc%d" % i, [128, 2, 512], BF16) for i in range(2)]
        mt = [cxs("mt%d" % i, [128, 512], F32) for i in range(8)]
        Xre_t = cxs("Xre", [128, NP_, 128], F32)
        Xim_t = cxs("Xim", [128, NP_, 128], F32)
        xrb = [Buf(Xre_t.t, "xrb%d" % j) for j in range(NP_)]
        xib = [Buf(Xim_t.t, "xib%d" % j) for j in range(NP_)]
        dt_ = [cxs("dt%d" % i, [128, 128], F32) for i in range(8)]
        et_ = [cxs("et%d" % i, [128, 128], F32) for i in range(4)]
        yT = [cxs("yT%d" % i, [128, UC, TB * 128], F32) for i in range(2)]
        gt = [cxs("gt%d" % i, [128, UC * TB * 128], F32) for i in range(2)]
        cx.op("dve", lambda e: e.memset(csb[0].t[:, :, :], 0.0), writes=[csb[0]])
        items = [(gi, blk, uc) for gi in range(NGRP) for blk in range(TB) for uc in range(UC)]
        cnt = {"d": 0}

        def stage_a(k):
            gi, blk, uc = items[k]
            par = k % 2
            uf, ub = uTf[gi % 2], uTb[gi % 2]
            t0 = gi * TB * 128
            if blk == 0 and uc == 0:
                cx.dma("sp", uf.t[:, :, :], UT.t[:, t0:t0 + TB * 128].rearrange("(u p) t -> p u t", p=128), reads=[UT],
                       writes=[uf])
                cx.dma("pool", ub.t[:, :, :], UT.t[:, t0:t0 + TB * 128].rearrange("(u p) t -> p u t", p=128), reads=[UT],
                       writes=[ub])
            o = blk * 128
            pr, pi_ = banks[2 * par], banks[2 * par + 1]
            cr, ci = banks[4 + 2 * par], banks[5 + 2 * par]
            W = Wc[par]
            for (pb, Bb) in ((pr, Bbre), (pi_, Bbim)):
                cx.op("pe", lambda e, pb=pb, Bb=Bb: e.matmul(
                    pb.t[:, 0:512], lhsT=ub.t[:, uc, o:o + 128], rhs=Bb.t[:, uc * 512:(uc + 1) * 512],
                    start=True, stop=True), reads=[ub, Bb], writes=[pb])
            cs_ = slice(uc * 512, (uc + 1) * 512)
            m = mt[4 * par:4 * par + 4]
            for (mi, pb, Em) in ((0, pr, Emre), (1, pi_, Emim), (2, pr, Emim), (3, pi_, Emre)):
                cx.op("dve", lambda e, mi=mi, pb=pb, Em=Em: e.tensor_tensor(
                    out=m[mi].t[:, :], in0=pb.t[:, 0:512], in1=Em.t[:, cs_], op=ALU.mult),
                    reads=[pb, Em], writes=[m[mi]])
            cx.op("pool", lambda e: e.tensor_tensor(out=W.t[:, 0, :], in0=m[0].t[:, :], in1=m[1].t[:, :],
                                                    op=ALU.subtract), reads=[m[0], m[1]], writes=[W])
            cx.op("pool", lambda e: e.tensor_tensor(out=W.t[:, 1, :], in0=m[2].t[:, :], in1=m[3].t[:, :],
                                                    op=ALU.add), reads=[m[2], m[3]], writes=[W])
            for jj in range(4):
                for (cbk, ri) in ((cr, 0), (ci, 1)):
                    cx.op("pe", lambda e, cbk=cbk, ri=ri, jj=jj: e.matmul(
                        cbk.t[:, jj * 128:(jj + 1) * 128], lhsT=W.t[:, ri, jj * 128:(jj + 1) * 128], rhs=trib.t[:, :],
                        start=True, stop=True), reads=[W, trib], writes=[cbk])

        def stage_b(k):
            gi, blk, uc = items[k]
            par = k % 2
            nblk = gi * TB + blk
            uf, yt = uTf[gi % 2], yT[gi % 2]
            o = blk * 128
            cr, ci = banks[4 + 2 * par], banks[5 + 2 * par]
            yb = banks[2 * par]
            cs_cur, cs_nxt = csb[nblk % 2], csb[(nblk + 1) % 2]
            for jj in range(4):
                j = uc * 4 + jj
                d = dt_[4 * (cnt["d"] % 2):4 * (cnt["d"] % 2) + 4]
                ev = et_[2 * (cnt["d"] % 2):2 * (cnt["d"] % 2) + 2]
                cnt["d"] += 1
                for (di, cbk, ri, Ep) in ((0, cr, 0, Epre), (1, ci, 1, Epim), (2, cr, 0, Epim), (3, ci, 1, Epre)):
                    cx.op("dve", lambda e, di=di, cbk=cbk, ri=ri, Ep=Ep: e.scalar_tensor_tensor(
                        out=d[di].t[:, :], in0=cbk.t[:, jj * 128:(jj + 1) * 128], scalar=cs_cur.t[:, ri, j:j + 1],
                        in1=Ep.t[:, j, :], op0=ALU.add, op1=ALU.mult), reads=[cbk, cs_cur, Ep], writes=[d[di]])
                cx.op("pool", lambda e: e.tensor_tensor(out=Xre_t.t[:, j, :], in0=d[0].t[:, :], in1=d[1].t[:, :],
                                                        op=ALU.subtract), reads=[d[0], d[1]], writes=[xrb[j]])
                cx.op("pool", lambda e: e.tensor_tensor(out=Xim_t.t[:, j, :], in0=d[2].t[:, :], in1=d[3].t[:, :],
                                                        op=ALU.add), reads=[d[2], d[3]], writes=[xib[j]])
            for jj in range(4):
                j = uc * 4 + jj
                cx.op("pe", lambda e, j=j, jj=jj: e.matmul(yb.t[:, 0:128], lhsT=Cre.t[:, j * 128:(j + 1) * 128],
                                                           rhs=Xre_t.t[:, j, :], start=(jj == 0), stop=False),
                      reads=[Cre, xrb[j]], writes=[yb])
                cx.op("pe", lambda e, j=j, jj=jj: e.matmul(yb.t[:, 0:128], lhsT=Cimn.t[:, j * 128:(j + 1) * 128],
                                                           rhs=Xim_t.t[:, j, :], start=False, stop=(jj == 3)),
                      reads=[Cimn, xib[j]], writes=[yb])
            cx.op("dve", lambda e: e.scalar_tensor_tensor(
                out=yt.t[:, uc, o:o + 128], in0=uf.t[:, uc, o:o + 128], scalar=dskip.t[:, uc:uc + 1],
                in1=yb.t[:, 0:128], op0=ALU.mult, op1=ALU.add), reads=[uf, dskip, yb], writes=[yt])
            if uc == UC - 1:
                sre, sim = Xre_t.t[:, :, 127], Xim_t.t[:, :, 127]
                cx.op("dve", lambda e: e.tensor_tensor(out=pp.t[:, 0, :], in0=abp.t[:, 0, :], in1=sre, op=ALU.mult),
                      reads=[abp] + xrb, writes=[pp])
                cx.op("dve", lambda e: e.tensor_tensor(out=pp.t[:, 1, :], in0=abp.t[:, 1, :], in1=sim, op=ALU.mult),
                      reads=[abp] + xib, writes=[pp])
                cx.op("dve", lambda e: e.tensor_tensor(out=cs_nxt.t[:, 0, :], in0=pp.t[:, 0, :], in1=pp.t[:, 1, :],
                                                       op=ALU.subtract), reads=[pp], writes=[cs_nxt])
                cx.op("dve", lambda e: e.tensor_tensor(out=pp.t[:, 2, :], in0=abp.t[:, 0, :], in1=sim, op=ALU.mult),
                      reads=[abp] + xib, writes=[pp])
                cx.op("dve", lambda e: e.tensor_tensor(out=pp.t[:, 3, :], in0=abp.t[:, 1, :], in1=sre, op=ALU.mult),
                      reads=[abp] + xrb, writes=[pp])
                cx.op("dve", lambda e: e.tensor_tensor(out=cs_nxt.t[:, 1, :], in0=pp.t[:, 2, :], in1=pp.t[:, 3, :],
                                                       op=ALU.add), reads=[pp], writes=[cs_nxt])
            if uc == UC - 1 and blk == TB - 1:
                t0 = gi * TB * 128
                yv = yt.t[:, :, :].rearrange("p u t -> p (u t)")
                g0, g1 = gt
                cx.op("act", lambda e: e.activation(out=g0.t[:, :], in_=yv, func=AF.Square), reads=[yt], writes=[g0])
                cx.op("dve", lambda e: e.tensor_scalar(out=g0.t[:, :], in0=g0.t[:, :], scalar1=0.044715, scalar2=1.0,
                                                       op0=ALU.mult, op1=ALU.add), reads=[g0], writes=[g0])
                cx.op("dve", lambda e: e.tensor_tensor(out=g0.t[:, :], in0=g0.t[:, :], in1=yv, op=ALU.mult), reads=[g0, yt],
                      writes=[g0])
                cx.op("act", lambda e: e.activation(out=g1.t[:, :], in_=g0.t[:, :], func=AF.Sigmoid,
                                                    scale=1.5957691216057308), reads=[g0], writes=[g1])
                cx.op("dve", lambda e: e.tensor_tensor(out=g1.t[:, :], in0=g1.t[:, :], in1=yv, op=ALU.mult), reads=[g1, yt],
                      writes=[g1])
                cx.dma("sp", zT_o.t[:, t0:t0 + TB * 128].rearrange("(u p) t -> p u t", p=128),
                       g1.t[:, :].rearrange("p (u t) -> p u t", u=UC), reads=[g1], writes=[zT_o], sembuf=g1)

        stage_a(0)
        for k in range(len(items)):
            if k + 1 < len(items):
                stage_a(k + 1)
            stage_b(k)
        cx.barrier()


def _vec_layout(v):
    return np.ascontiguousarray(np.asarray(v, np.float32).reshape(-1, 128).T)


def _interleave(a, b):
    K = a.shape[0]
    n = a.shape[1] // 128
    return np.ascontiguousarray(np.stack([a.reshape(K, n, 128), b.reshape(K, n, 128)], axis=2).reshape(K, n * 256))


def _rep(v):
    return np.ascontiguousarray(np.broadcast_to(np.asarray(v, np.float32).reshape(1, -1), (128, np.size(v))))


def _const_tables(NH, NQB, slopes):
    p = np.arange(128, dtype=np.float32)
    kb = np.zeros((128, NH, NQB), np.float32)
    for h in range(NH):
        for d in range(NQB):
            kb[:, h, d] = slopes[h] * (p - 64.0 - 128.0 * d)
    cm = (p[:, None] <= p[None, :]).astype(np.float32)
    return dict(kbias=kb.reshape(128, NH * NQB), cmask=np.concatenate([cm, cm], 1), ident=np.eye(128, dtype=np.float32),
                iota_p=p.reshape(128, 1).copy(), iota_f=_rep(p), tri=cm.copy())


def _s5_layout(a_re, a_im, b_re, b_im, c_re, c_im, d_skip, log_dt):
    NG = a_re.shape[0]
    UC, NP_ = NG // 8, NG // 2
    arep = np.concatenate([_rep(a_re.reshape(-1)), _rep(a_im.reshape(-1)), _rep(np.repeat(log_dt, 64))], 1)
    ap = np.zeros((128, 3, NP_), np.float32)
    for j in range(NP_):
        for g2 in range(2):
            g = 2 * j + g2
            ap[g2 * 64:(g2 + 1) * 64, 0, j] = a_re[g]
            ap[g2 * 64:(g2 + 1) * 64, 1, j] = a_im[g]
            ap[g2 * 64:(g2 + 1) * 64, 2, j] = log_dt[g]
    Bre = np.zeros((128, UC, 8, 64), np.float32)
    Bim = np.zeros((128, UC, 8, 64), np.float32)
    Cre = np.zeros((128, NP_, 128), np.float32)
    Cim = np.zeros((128, NP_, 128), np.float32)
    for g in range(NG):
        uc, g8 = g // 8, g % 8
        Bre[g8 * 16:(g8 + 1) * 16, uc, g8, :] = b_re[g].T
        Bim[g8 * 16:(g8 + 1) * 16, uc, g8, :] = b_im[g].T
        j, g2 = g // 2, g % 2
        Cre[g2 * 64:(g2 + 1) * 64, j, g8 * 16:(g8 + 1) * 16] = c_re[g].T
        Cim[g2 * 64:(g2 + 1) * 64, j, g8 * 16:(g8 + 1) * 16] = c_im[g].T
    return dict(arep=arep, apair=ap.reshape(128, 3 * NP_), Bre_big=Bre.reshape(128, -1), Bim_big=Bim.reshape(128, -1),
                Cre_pad=Cre.reshape(128, -1), Cim_pad=Cim.reshape(128, -1), dskipT=_vec_layout(d_skip))


_D, _L, _B, _F = 4096, 8192, 2, 11008
_NCORE = 8
_CACHE = {}


def _get(name, builder, P):
    if name not in _CACHE:
        _CACHE[name] = builder(P)
    return _CACHE[name]


def kernel(x, c, ada_w, ada_b, norm1_g, w_in, lq1, lk1, lq2, lk2, subln_g,
           a_re, a_im, b_re, b_im, c_re, c_im, d_skip, log_dt, w_glu,
           w_attn_br, w_ssm_br, w_out, norm2_g, w_up, conv_w, conv_b, w_down, final_g):
    f32 = lambda a: np.ascontiguousarray(np.asarray(a, dtype=np.float32))
    x, c = f32(x), f32(c)
    w_in0 = np.asarray(w_in, np.float32)[0]
    NH, NG = 4, 32
    HWc, UWc = NH * 128, NG * 16
    NQB = _L // 128
    lam4 = _rep(np.concatenate([np.asarray(v, np.float32).reshape(-1) for v in (lq1, lk1, lq2, lk2)]))
    subg = _rep(np.asarray(subln_g, np.float32)[0])
    ada_w0 = f32(np.asarray(ada_w)[0])
    ada_bT = _vec_layout(np.asarray(ada_b)[0])
    n1gT = _vec_layout(np.asarray(norm1_g)[0])
    xT_host = [np.ascontiguousarray(x[b].T) for b in range(_B)]
    in1 = []
    wcache = {}
    for i in range(_NCORE):
        b, hg = i // 4, i % 4
        if hg not in wcache:
            slopes = np.exp2(-8.0 * (np.arange(hg * NH, (hg + 1) * NH) + 1) / 16.0).astype(np.float32)
            gs = slice(hg * NG, (hg + 1) * NG)
            d = dict(
                w_q=f32(w_in0[:, hg * HWc:(hg + 1) * HWc]),
                w_k=f32(w_in0[:, 2048 + hg * HWc:2048 + (hg + 1) * HWc]),
                w_v=f32(w_in0[:, 4096 + hg * HWc:4096 + (hg + 1) * HWc]),
                w_u=f32(w_in0[:, 6144 + hg * UWc:6144 + (hg + 1) * UWc]),
            )
            d.update(_const_tables(NH, NQB, slopes))
            d.update(_s5_layout(*[np.asarray(v, np.float32)[0][gs] for v in (a_re, a_im, b_re, b_im, c_re, c_im)],
                                np.asarray(d_skip, np.float32)[0][hg * UWc:(hg + 1) * UWc],
                                np.asarray(log_dt, np.float32)[0][gs]))
            wcache[hg] = d
        m = dict(xbT=xT_host[b], cT=_vec_layout(c[b]), ada_w=ada_w0, ada_bT=ada_bT, n1g=n1gT, lam4=lam4, subg=subg)
        m.update(wcache[hg])
        in1.append(m)
    nc1 = _get("mixer", build_mixer, dict(D=_D, L=_L, NH=NH, NG=NG))
    r1 = run_bass_kernel_spmd(nc1, in1, core_ids=list(range(_NCORE))).results
    del in1, wcache
    AT = [np.concatenate([np.asarray(r1[b * 4 + hg]["A_out"]).T for hg in range(4)], 0) for b in range(_B)]
    ZT = [np.concatenate([np.asarray(r1[b * 4 + hg]["zT"]) for hg in range(4)], 0) for b in range(_B)]
    modT = [np.asarray(r1[b * 4]["modT"]) for b in range(_B)]
    del r1
    TOKC = _L // 4
    n2gT = _vec_layout(np.asarray(norm2_g)[0])
    fgT = _vec_layout(np.asarray(final_g))
    cwl = np.concatenate([_vec_layout(np.asarray(conv_w, np.float32)[0][j]) for j in range(3)], 1)
    w_up0 = np.asarray(w_up, np.float32)[0]
    shared = dict(w_gate=_interleave(w_in0[:, 8192:12288], w_in0[:, 12288:16384]), w_glu=f32(np.asarray(w_glu)[0]), w_a=f32(np.asarray(w_attn_br)[0]),
                  w_s=f32(np.asarray(w_ssm_br)[0]), w_out=f32(np.asarray(w_out)[0]), w_up=_interleave(w_up0[:, :_F], w_up0[:, _F:]),
                  cw=cwl, cb=_vec_layout(np.asarray(conv_b)[0]), w_down=f32(np.asarray(w_down)[0]),
                  ident=np.eye(128, dtype=np.float32))
    in2 = []
    for i in range(_NCORE):
        b, j = i // 4, i % 4
        lo = j * TOKC - 2

        def halo(a, axis):
            if lo >= 0:
                return f32(np.take(a, np.arange(lo, lo + TOKC + 2), axis=axis))
            body = np.take(a, np.arange(0, TOKC), axis=axis)
            shp = list(body.shape)
            shp[axis] = 2
            return f32(np.concatenate([np.zeros(shp, np.float32), body], axis))
        m = dict(xsT=halo(xT_host[b], 1), AT=halo(AT[b], 1), ZT=halo(ZT[b], 1),
                 vecs=np.ascontiguousarray(np.concatenate([modT[b], n1gT, n2gT, fgT], 1)),
                 hmask=np.full((128, 1), 1.0 if j > 0 else 0.0, np.float32))
        m.update(shared)
        in2.append(m)
    nc2 = _get("tail", build_tail, dict(D=_D, F=_F, AW=2048, SW=2048, TOK=TOKC + 2, TT=410, FG=22))
    r2 = run_bass_kernel_spmd(nc2, in2, core_ids=list(range(_NCORE))).results
    out = np.empty((_B, _L, _D), np.float32)
    for i in range(_NCORE):
        b, j = i // 4, i % 4
        out[b, j * TOKC:(j + 1) * TOKC] = np.asarray(r2[i]["outT"]).T
    return out
```

```python
import numpy as np
from contextlib import ExitStack
import concourse.bass as bass
import concourse.mybir as mybir
from concourse.bass_utils import run_bass_kernel_spmd

F32 = mybir.dt.float32
BF16 = mybir.dt.bfloat16
AF = mybir.ActivationFunctionType
ALU = mybir.AluOpType


class Buf:
    __slots__ = ("t", "w", "r", "dsem", "dcnt", "name")

    def __init__(self, t, name):
        self.t = t
        self.w = None
        self.r = {}
        self.dsem = None
        self.dcnt = 0
        self.name = name

    def __getitem__(self, idx):
        return self.t[idx]


class Ctx:
    def __init__(self, nc, es):
        self.nc = nc
        self.es = es
        self.E = {"pe": nc.tensor, "act": nc.scalar, "dve": nc.vector,
                  "pool": nc.gpsimd, "sp": nc.sync}
        self.sem = {}
        self.cnt = {}
        for e in ("pe", "act", "dve", "pool"):
            self.sem[e] = es.enter_context(nc.semaphore("s_" + e))
            self.cnt[e] = 0
        self.waited = {}
        self.nbuf = 0
        self.dbufs = []

    def sb(self, name, shape, dt):
        return Buf(self.es.enter_context(self.nc.sbuf_tensor(name + "_s", list(shape), dt)), name)

    def ps(self, name, shape, dt=F32):
        return Buf(self.es.enter_context(self.nc.psum_tensor(name, list(shape), dt)), name)

    def dram(self, name, shape, dt):
        return Buf(self.nc.dram_tensor(name, list(shape), dt).ap(), name)

    def ext(self, name, shape, dt, out=False):
        return Buf(self.nc.dram_tensor(name, list(shape), dt,
                                       kind="ExternalOutput" if out else "ExternalInput").ap(), name)

    def _wait(self, e, ticks, force_self=False):
        for (sem, val, key) in ticks:
            if e == "pe" and key == "pe" and not force_self:
                continue
            if self.waited.get((e, key), 0) < val:
                self.E[e].wait_ge(sem, val)
                self.waited[(e, key)] = val

    @staticmethod
    def _deps(reads, writes):
        t = []
        for b in reads:
            if b.w is not None:
                t.append(b.w)
        for b in writes:
            if b.w is not None:
                t.append(b.w)
            t.extend(b.r.values())
        return t

    @staticmethod
    def _reg(tk, reads, writes):
        for b in reads:
            b.r[tk[2]] = tk
        for b in writes:
            b.w = tk
            b.r = {}

    def op(self, e, fn, reads=(), writes=(), force_self=False):
        self._wait(e, self._deps(reads, writes), force_self)
        inst = fn(self.E[e])
        self.cnt[e] += 1
        inst.then_inc(self.sem[e], 1)
        self._reg((self.sem[e], self.cnt[e], e), reads, writes)

    def dma(self, q, out_ap, in_ap, reads=(), writes=(), sembuf=None, **kw):
        self._wait(q, self._deps(reads, writes))
        sbf = sembuf if sembuf is not None else writes[0]
        if sbf.dsem is None:
            self.nbuf += 1
            sbf.dsem = self.es.enter_context(self.nc.semaphore("d%d_%s" % (self.nbuf, sbf.name)))
            self.dbufs.append(sbf)
        inst = self.E[q].dma_start(out=out_ap, in_=in_ap, **kw)
        sbf.dcnt += 16
        inst.then_inc(sbf.dsem, 16)
        self._reg((sbf.dsem, sbf.dcnt, "d_" + sbf.name), reads, writes)

    def barrier(self):
        ticks = [(self.sem[e], self.cnt[e], e) for e in ("pe", "act", "dve", "pool") if self.cnt[e] > 0]
        ticks += [(b.dsem, b.dcnt, "d_" + b.name) for b in self.dbufs]
        for e in ("sp", "pool", "act", "dve", "pe"):
            self._wait(e, ticks)

    def finish(self, bufs):
        ticks = []
        for b in bufs:
            if b.w is not None:
                ticks.append(b.w)
            ticks.extend(b.r.values())
        self._wait("sp", ticks)


def _subtiles(n):
    r = []
    o = 0
    while o < n:
        r.append((o, min(128, n - o)))
        o += 128
    return r


def build_tail(P):
    D, F, AW, SW, TOK, TT = P["D"], P["F"], P["AW"], P["SW"], P["TOK"], P["TT"]
    KC, FC, AC, SC = D // 128, F // 128, AW // 128, SW // 128
    FG = P.get("FG", 22)
    EPS = 1e-6
    nc = bass.Bass("TRN2", target_bir_lowering=False)
    es = ExitStack()
    with es:
        cx = Ctx(nc, es)
        xsT = cx.ext("xsT", [D, TOK], F32)
        ATd = cx.ext("AT", [AW, TOK], F32)
        ZTd = cx.ext("ZT", [SW, TOK], F32)
        vecs_d = cx.ext("vecs", [128, 9 * KC], F32)
        w_gate = cx.ext("w_gate", [D, 2 * D], F32)
        w_glu = cx.ext("w_glu", [SW, SW], F32)
        w_a = cx.ext("w_a", [AW, D], F32)
        w_s = cx.ext("w_s", [SW, D], F32)
        w_out = cx.ext("w_out", [D, D], F32)
        w_up = cx.ext("w_up", [D, 2 * F], F32)
        cw_d = cx.ext("cw", [128, 3 * 2 * FC], F32)
        cb_d = cx.ext("cb", [128, 2 * FC], F32)
        w_down = cx.ext("w_down", [F, D], F32)
        hmask_d = cx.ext("hmask", [128, 1], F32)
        ident_d = cx.ext("ident", [128, 128], F32)
        outT = cx.ext("outT", [D, TOK - 2], F32, out=True)

        WS = 8192
        wslot = [cx.sb("wslot%d" % i, [128, WS], BF16) for i in range(2)]
        stage = [cx.sb("stage%d" % i, [128, 1024], F32) for i in range(2)]
        xT = cx.sb("xT", [128, KC, TT], F32)
        hT = cx.sb("hT", [128, KC, TT], BF16)
        r1n = max((AC + SC) * TT, FG * TT)
        R1 = cx.sb("R1", [128, r1n], BF16)
        ZM = cx.sb("ZM", [128, KC, TT], BF16)
        vecs = cx.sb("vecs_sb", [128, 9 * KC], F32)
        G1 = cx.sb("G1", [128, KC], F32)
        G2 = cx.sb("G2", [128, KC], F32)
        cw = cx.sb("cw_sb", [128, 3 * 2 * FC], F32)
        cb = cx.sb("cb_sb", [128, 2 * FC], F32)
        carry = cx.sb("carry", [128, 2 * FC, 2], F32)
        hmask = cx.sb("hmask_sb", [128, 1], F32)
        identf = cx.sb("identf", [128, 128], F32)
        onesb = cx.sb("onesb", [128, 128], BF16)
        epsb = cx.sb("epsb", [128, 1], F32)
        zcol = cx.sb("zcol", [128, 2], F32)
        upb = [cx.sb("upb%d" % i, [128, TT + 2], F32) for i in range(4)]
        tmpf = [cx.sb("tmpf%d" % i, [128, TT], F32) for i in range(6)]
        sqb = [cx.sb("sqb%d" % i, [128, TT], BF16) for i in range(2)]
        rstd = cx.sb("rstd", [128, TT], F32)
        banks = [cx.ps("bank%d" % i, [128, 512], F32) for i in range(8)]
        st = {"bank": 0, "slot": 0, "tmp": 0, "stage": 0, "ti": 0, "wl": 0}

        def bank():
            b = banks[st["bank"] % 8]
            st["bank"] += 1
            return b

        def tmp():
            b = tmpf[st["tmp"] % 6]
            st["tmp"] += 1
            return b

        NLMAX = P.get("NL", 256)
        ntiles = (TOK + TT - 1) // TT
        wscs = [cx.dram("wsc%d" % i, [32, 128, WS], BF16) for i in range((NLMAX + 31) // 32)]

        def wload(pieces):
            s = wslot[st["slot"] % 2]
            st["slot"] += 1
            idx = st["wl"]
            st["wl"] += 1
            assert idx < NLMAX
            views = []
            off = 0
            ctile = 1 if (idx % 3 == 2 and ntiles > 2) else 0
            for (wb, r0, nr, c0, ncl) in pieces:
                kc = nr // 128
                v = s.t[:, off:off + kc * ncl].rearrange("p (k n) -> p k n", n=ncl)
                if st["ti"] <= ctile:
                    src = wb.t[r0:r0 + nr, c0:c0 + ncl].rearrange("(k p) n -> p k n", p=128)
                    cx.dma("pool", v, src, reads=[wb], writes=[s])
                views.append(v)
                off += kc * ncl
            assert off <= WS
            wsc = wscs[idx // 32]
            if st["ti"] == ctile:
                cx.dma("sp", wsc.t[idx % 32, :, 0:off], s.t[:, 0:off], reads=[s], writes=[wsc], sembuf=s)
            elif st["ti"] > ctile:
                cx.dma("sp", s.t[:, 0:off], wsc.t[idx % 32, :, 0:off], reads=[wsc], writes=[s])
            return s, views

        cx.dma("sp", vecs.t[:, :], vecs_d.t[:, :], reads=[vecs_d], writes=[vecs])
        cx.dma("sp", cw.t[:, :], cw_d.t[:, :], reads=[cw_d], writes=[cw])
        cx.dma("sp", cb.t[:, :], cb_d.t[:, :], reads=[cb_d], writes=[cb])
        cx.dma("sp", hmask.t[:, :], hmask_d.t[:, :], reads=[hmask_d], writes=[hmask])
        cx.dma("sp", identf.t[:, :], ident_d.t[:, :], reads=[ident_d], writes=[identf])
        cx.op("dve", lambda e: e.memset(onesb.t[:, :], 1.0), writes=[onesb])
        cx.op("dve", lambda e: e.memset(epsb.t[:, :], EPS), writes=[epsb])
        cx.op("dve", lambda e: e.memset(zcol.t[:, :], 0.0), writes=[zcol])

        def vcol(s, c):
            return vecs.t[:, s * KC + c:s * KC + c + 1]
        cx.op("dve", lambda e: e.scalar_tensor_tensor(out=G1.t[:, :], in0=vecs.t[:, 1 * KC:2 * KC], scalar=1.0,
                                                     in1=vecs.t[:, 6 * KC:7 * KC], op0=ALU.add, op1=ALU.mult),
              reads=[vecs], writes=[G1])
        cx.op("dve", lambda e: e.scalar_tensor_tensor(out=G2.t[:, :], in0=vecs.t[:, 4 * KC:5 * KC], scalar=1.0,
                                                     in1=vecs.t[:, 7 * KC:8 * KC], op0=ALU.add, op1=ALU.mult),
              reads=[vecs], writes=[G2])

        def fnorm(n, Gap, shift_fn, outbuf):
            ssb = bank()
            for c in range(KC):
                sq = sqb[c % 2]
                cx.op("act", lambda e, c=c, sq=sq: e.activation(out=sq.t[:, :n], in_=xT.t[:, c, :n], func=AF.Square),
                      reads=[xT], writes=[sq])
                cx.op("pe", lambda e, c=c, sq=sq: e.matmul(ssb.t[:, :n], lhsT=onesb.t[:, :], rhs=sq.t[:, :n],
                                                         start=(c == 0), stop=(c == KC - 1)),
                      reads=[onesb, sq], writes=[ssb])
            cx.op("act", lambda e: e.activation(out=rstd.t[:, :n], in_=ssb.t[:, :n], func=AF.Sqrt,
                                                bias=epsb.t[:, 0:1], scale=1.0 / D),
                  reads=[ssb, epsb], writes=[rstd])
            cx.op("dve", lambda e: e.reciprocal(out=rstd.t[:, :n], in_=rstd.t[:, :n]), reads=[rstd], writes=[rstd])
            for c in range(KC):
                t = tmp()
                cx.op("dve", lambda e, c=c, t=t: e.tensor_tensor(out=t.t[:, :n], in0=xT.t[:, c, :n], in1=rstd.t[:, :n],
                                                               op=ALU.mult),
                      reads=[xT, rstd], writes=[t])
                bias = shift_fn(c) if shift_fn is not None else zcol.t[:, 0:1]
                cx.op("act", lambda e, c=c, t=t, bias=bias: e.activation(out=outbuf.t[:, c, :n], in_=t.t[:, :n],
                                                                        func=AF.Identity, bias=bias,
                                                                        scale=Gap[:, c:c + 1]),
                      reads=[t, vecs, G1, G2, zcol], writes=[outbuf])

        for ti in range(ntiles):
            t0 = ti * TT
            n = min(TT, TOK - t0)
            subs = _subtiles(n)
            st["ti"] = ti
            st["wl"] = 0
            for c8 in range(0, KC, 8):
                ce = min(KC, c8 + 8)
                cx.dma("sp", xT.t[:, c8:ce, 0:n], xsT.t[c8 * 128:ce * 128, t0:t0 + n].rearrange("(c p) t -> p c t", p=128),
                       reads=[xsT], writes=[xT])
            fnorm(n, G1.t, lambda c: vcol(0, c), hT)
            ATv = R1.t[:, 0:AC * TT].rearrange("p (c t) -> p c t", t=TT)
            STv = R1.t[:, AC * TT:(AC + SC) * TT].rearrange("p (c t) -> p c t", t=TT)
            cx.dma("pool", ATv[:, :, 0:n], ATd.t[:, t0:t0 + n].rearrange("(c p) t -> p c t", p=128),
                   reads=[ATd], writes=[R1])
            cx.dma("pool", ZM.t[:, 0:SC, 0:n], ZTd.t[:, t0:t0 + n].rearrange("(c p) t -> p c t", p=128),
                   reads=[ZTd], writes=[ZM])
            gcols = min(SW, WS // SC // 128 * 128)
            for g0 in range(0, SW, gcols):
                s, (wv,) = wload([(w_glu, 0, SW, g0, gcols)])
                for j in range(gcols // 128):
                    nch = g0 // 128 + j
                    pb = bank()
                    for k in range(SC):
                        cx.op("pe", lambda e, k=k, j=j, pb=pb, wv=wv: e.matmul(
                            pb.t[:, :n], lhsT=wv[:, k, j * 128:(j + 1) * 128], rhs=ZM.t[:, k, :n],
                            start=(k == 0), stop=(k == SC - 1)), reads=[s, ZM], writes=[pb])
                    t = tmp()
                    cx.op("act", lambda e, pb=pb, t=t: e.activation(out=t.t[:, :n], in_=pb.t[:, :n], func=AF.Sigmoid),
                          reads=[pb], writes=[t])
                    cx.op("dve", lambda e, t=t, nch=nch: e.tensor_tensor(out=STv[:, nch, :n], in0=ZM.t[:, nch, :n],
                                                                       in1=t.t[:, :n], op=ALU.mult),
                          reads=[ZM, t], writes=[R1])
            for nch in range(KC):
                s1, (wgg,) = wload([(w_gate, 0, D, nch * 256, 256)])
                wga, wgs = wgg[:, :, 0:128], wgg[:, :, 128:256]
                s2, (wa, wsb) = wload([(w_a, 0, AW, nch * 128, 128), (w_s, 0, SW, nch * 128, 128)])
                pga, pgs, pa, pss = bank(), bank(), bank(), bank()
                for (pb, wv, s, src, kcn) in ((pga, wga, s1, hT, KC), (pgs, wgs, s1, hT, KC)):
                    for k in range(kcn):
                        cx.op("pe", lambda e, k=k, pb=pb, wv=wv, src=src, kcn=kcn: e.matmul(
                            pb.t[:, :n], lhsT=wv[:, k, :], rhs=src.t[:, k, :n], start=(k == 0), stop=(k == kcn - 1)),
                            reads=[s, src], writes=[pb])
                for (pb, wv, srcv, kcn) in ((pa, wa, ATv, AC), (pss, wsb, STv, SC)):
                    for k in range(kcn):
                        cx.op("pe", lambda e, k=k, pb=pb, wv=wv, srcv=srcv, kcn=kcn: e.matmul(
                            pb.t[:, :n], lhsT=wv[:, k, :], rhs=srcv[:, k, :n], start=(k == 0), stop=(k == kcn - 1)),
                            reads=[s2, R1], writes=[pb])
                ta, tb, tc, td = tmp(), tmp(), tmp(), tmp()
                cx.op("act", lambda e: e.activation(out=ta.t[:, :n], in_=pga.t[:, :n], func=AF.Sigmoid),
                      reads=[pga], writes=[ta])
                cx.op("act", lambda e: e.activation(out=tb.t[:, :n], in_=pgs.t[:, :n], func=AF.Sigmoid),
                      reads=[pgs], writes=[tb])
                cx.op("dve", lambda e: e.tensor_tensor(out=tc.t[:, :n], in0=ta.t[:, :n], in1=pa.t[:, :n], op=ALU.mult),
                      reads=[ta, pa], writes=[tc])
                cx.op("dve", lambda e: e.tensor_tensor(out=td.t[:, :n], in0=tb.t[:, :n], in1=pss.t[:, :n], op=ALU.mult),
                      reads=[tb, pss], writes=[td])
                cx.op("dve", lambda e: e.tensor_tensor(out=ZM.t[:, nch, :n], in0=tc.t[:, :n], in1=td.t[:, :n],
                                                       op=ALU.add), reads=[tc, td], writes=[ZM])
            for n0 in range(0, KC, 2):
                ncn = min(2, KC - n0)
                s, (wv,) = wload([(w_out, 0, D, n0 * 128, ncn * 128)])
                for j in range(ncn):
                    nch = n0 + j
                    pb = bank()
                    for k in range(KC):
                        cx.op("pe", lambda e, k=k, j=j, pb=pb, wv=wv: e.matmul(
                            pb.t[:, :n], lhsT=wv[:, k, j * 128:(j + 1) * 128], rhs=ZM.t[:, k, :n],
                            start=(k == 0), stop=(k == KC - 1)), reads=[s, ZM], writes=[pb])
                    cx.op("dve", lambda e, pb=pb, nch=nch: e.scalar_tensor_tensor(
                        out=xT.t[:, nch, :n], in0=pb.t[:, :n], scalar=vcol(2, nch), in1=xT.t[:, nch, :n],
                        op0=ALU.mult, op1=ALU.add), reads=[pb, vecs, xT], writes=[xT])
            fnorm(n, G2.t, lambda c: vcol(3, c), hT)
            ACTv = R1.t[:, 0:FG * TT].rearrange("p (c t) -> p c t", t=TT)
            for f0 in range(0, FC, FG):
                fgn = min(FG, FC - f0)
                for fi in range(fgn):
                    fc = f0 + fi
                    s, (wuu,) = wload([(w_up, 0, D, fc * 256, 256)])
                    wua, wug = wuu[:, :, 0:128], wuu[:, :, 128:256]
                    res = []
                    for kind, wv in ((0, wua), (1, wug)):
                        ch = kind * FC + fc
                        pb = bank()
                        for k in range(KC):
                            cx.op("pe", lambda e, k=k, pb=pb, wv=wv: e.matmul(
                                pb.t[:, :n], lhsT=wv[:, k, :], rhs=hT.t[:, k, :n], start=(k == 0), stop=(k == KC - 1)),
                                reads=[s, hT], writes=[pb])
                        ub = upb[(2 * fc + kind) % 4]
                        cx.op("act", lambda e, pb=pb, ub=ub: e.activation(out=ub.t[:, 2:2 + n], in_=pb.t[:, :n],
                                                                          func=AF.Identity), reads=[pb], writes=[ub])
                        if ti == 0:
                            cx.op("dve", lambda e, ub=ub: e.tensor_scalar(out=ub.t[:, 2:4], in0=ub.t[:, 2:4],
                                                                          scalar1=hmask.t[:, 0:1], scalar2=None,
                                                                          op0=ALU.mult), reads=[ub, hmask], writes=[ub])
                            cx.op("dve", lambda e, ub=ub: e.tensor_copy(out=ub.t[:, 0:2], in_=zcol.t[:, 0:2]),
                                  reads=[zcol], writes=[ub])
                        else:
                            cx.op("dve", lambda e, ub=ub, ch=ch: e.tensor_copy(out=ub.t[:, 0:2], in_=carry.t[:, ch, :]),
                                  reads=[carry], writes=[ub])
                        cx.op("dve", lambda e, ub=ub, ch=ch: e.tensor_copy(out=carry.t[:, ch, :], in_=ub.t[:, n:n + 2]),
                              reads=[ub], writes=[carry])
                        c0, c1, c2 = tmp(), tmp(), tmp()
                        cx.op("act", lambda e, ub=ub, ch=ch, c0=c0: e.activation(
                            out=c0.t[:, :n], in_=ub.t[:, 2:2 + n], func=AF.Identity, bias=cb.t[:, ch:ch + 1],
                            scale=cw.t[:, 2 * 2 * FC + ch:2 * 2 * FC + ch + 1]), reads=[ub, cb, cw], writes=[c0])
                        cx.op("dve", lambda e, ub=ub, ch=ch, c0=c0, c1=c1: e.scalar_tensor_tensor(
                            out=c1.t[:, :n], in0=ub.t[:, 1:1 + n], scalar=cw.t[:, 2 * FC + ch:2 * FC + ch + 1],
                            in1=c0.t[:, :n], op0=ALU.mult, op1=ALU.add), reads=[ub, cw, c0], writes=[c1])
                        cx.op("dve", lambda e, ub=ub, ch=ch, c1=c1, c2=c2: e.scalar_tensor_tensor(
                            out=c2.t[:, :n], in0=ub.t[:, 0:n], scalar=cw.t[:, ch:ch + 1],
                            in1=c1.t[:, :n], op0=ALU.mult, op1=ALU.add), reads=[ub, cw, c1], writes=[c2])
                        res.append(c2)
                    ca, cg = res
                    sg_ = tmp()
                    cx.op("act", lambda e, cg=cg, sg_=sg_: e.activation(out=sg_.t[:, :n], in_=cg.t[:, :n], func=AF.Silu),
                          reads=[cg], writes=[sg_])
                    cx.op("dve", lambda e, ca=ca, sg_=sg_, fi=fi: e.tensor_tensor(
                        out=ACTv[:, fi, :n], in0=sg_.t[:, :n], in1=ca.t[:, :n], op=ALU.mult),
                        reads=[sg_, ca], writes=[R1])
                for n0 in range(0, KC, 2):
                    ncn = min(2, KC - n0)
                    s, (wv,) = wload([(w_down, f0 * 128, fgn * 128, n0 * 128, ncn * 128)])
                    for j in range(ncn):
                        nch = n0 + j
                        pb = bank()
                        for k in range(fgn):
                            cx.op("pe", lambda e, k=k, j=j, pb=pb, wv=wv: e.matmul(
                                pb.t[:, :n], lhsT=wv[:, k, j * 128:(j + 1) * 128], rhs=ACTv[:, k, :n],
                                start=(k == 0), stop=(k == fgn - 1)), reads=[s, R1], writes=[pb])
                        cx.op("dve", lambda e, pb=pb, nch=nch: e.scalar_tensor_tensor(
                            out=xT.t[:, nch, :n], in0=pb.t[:, :n], scalar=vcol(5, nch), in1=xT.t[:, nch, :n],
                            op0=ALU.mult, op1=ALU.add), reads=[pb, vecs, xT], writes=[xT])
            fnorm(n, vecs.t[:, 8 * KC:9 * KC], None, xT)
            p0 = 2 if ti == 0 else 0
            for c8 in range(0, KC, 8):
                ce = min(KC, c8 + 8)
                cx.dma("sp", outT.t[c8 * 128:ce * 128, t0 + p0 - 2:t0 + n - 2].rearrange("(c p) t -> p c t", p=128),
                       xT.t[:, c8:ce, p0:n], reads=[xT], writes=[outT], sembuf=xT)
        cx.barrier()
    return nc


def build_mixer(P):
    D, L, NH, NG = P["D"], P["L"], P["NH"], P["NG"]
    LAM_INIT = 0.2
    KC = D // 128
    TT = min(512, L)
    NQB = L // 128
    HW = NH * 128
    UW = NG * 16
    UC = UW // 128
    NP = NG // 2
    EPS = 1e-6
    PI = float(np.pi)
    nc = bass.Bass("TRN2", target_bir_lowering=False)
    es = ExitStack()
    with es:
        cx = Ctx(nc, es)
        xbT = cx.ext("xbT", [D, L], F32)
        cT_d = cx.ext("cT", [128, KC], F32)
        ada_w = cx.ext("ada_w", [D, 6 * D], F32)
        ada_bT = cx.ext("ada_bT", [128, 6 * KC], F32)
        n1g_d = cx.ext("n1g", [128, KC], F32)
        w_q = cx.ext("w_q", [D, HW], F32)
        w_k = cx.ext("w_k", [D, HW], F32)
        w_v = cx.ext("w_v", [D, HW], F32)
        w_u = cx.ext("w_u", [D, UW], F32)
        lam4_d = cx.ext("lam4", [128, 256], F32)
        subg_d = cx.ext("subg", [128, 128], F32)
        kbias_d = cx.ext("kbias", [128, NH * NQB], F32)
        cmask_d = cx.ext("cmask", [128, 256], F32)
        ident_d = cx.ext("ident", [128, 128], F32)
        arep_d = cx.ext("arep", [128, 3 * NG * 64], F32)
        apair_d = cx.ext("apair", [128, 3 * NP], F32)
        Bre_d = cx.ext("Bre_big", [128, UC * 512], F32)
        Bim_d = cx.ext("Bim_big", [128, UC * 512], F32)
        Cre_d = cx.ext("Cre_pad", [128, NP * 128], F32)
        Cim_d = cx.ext("Cim_pad", [128, NP * 128], F32)
        dskip_d = cx.ext("dskipT", [128, UC], F32)
        iotap_d = cx.ext("iota_p", [128, 1], F32)
        iotaf_d = cx.ext("iota_f", [128, 128], F32)
        tri_d = cx.ext("tri", [128, 128], F32)
        modT_o = cx.ext("modT", [128, 6 * KC], F32, out=True)
        A_o = cx.ext("A_out", [L, HW], F32, out=True)
        zT_o = cx.ext("zT", [UW, L], F32, out=True)
        dbg_o = cx.ext("dbg", [128, 4096], F32, out=True) if P.get("dbg") else None
        QT = cx.dram("QT", [HW, L], BF16)
        KT = cx.dram("KT", [HW, L], BF16)
        Vd = cx.dram("Vd", [L, HW], BF16)
        UT = cx.dram("UT", [UW, L], F32)

        WS = 8192
        wslot = [cx.sb("wslot%d" % i, [128, WS], BF16) for i in range(2)]
        stage = [cx.sb("stage%d" % i, [128, 1024], F32) for i in range(2)]
        identf = cx.sb("identf", [128, 128], F32)
        onesb = cx.sb("onesb", [128, 128], BF16)
        epsb = cx.sb("epsb", [128, 1], F32)
        zcol = cx.sb("zcol", [128, 2], F32)
        modT = cx.sb("modT_sb", [128, 6 * KC], F32)
        n1g = cx.sb("n1g_sb", [128, KC], F32)
        G1 = cx.sb("G1", [128, KC], F32)
        banks = [cx.ps("bank%d" % i, [128, 512], F32) for i in range(8)]
        st = {"bank": 0, "slot": 0, "stage": 0}

        def bank():
            b = banks[st["bank"] % 8]
            st["bank"] += 1
            return b

        def wload(pieces):
            slots = st.get("slots", wslot)
            s = slots[st["slot"] % len(slots)]
            st["slot"] += 1
            views = []
            off = 0
            for (wb, r0, nr, c0, ncl) in pieces:
                kc = nr // 128
                v = s.t[:, off:off + kc * ncl].rearrange("p (k n) -> p k n", n=ncl)
                src = wb.t[r0:r0 + nr, c0:c0 + ncl].rearrange("(k p) n -> p k n", p=128)
                cx.dma("pool", v, src, reads=[wb], writes=[s])
                views.append(v)
                off += kc * ncl
            assert off <= WS
            return s, views

        cx.dma("sp", identf.t[:, :], ident_d.t[:, :], reads=[ident_d], writes=[identf])
        cx.dma("sp", n1g.t[:, :], n1g_d.t[:, :], reads=[n1g_d], writes=[n1g])
        cx.op("dve", lambda e: e.memset(onesb.t[:, :], 1.0), writes=[onesb])
        cx.op("dve", lambda e: e.memset(epsb.t[:, :], EPS), writes=[epsb])
        cx.op("dve", lambda e: e.memset(zcol.t[:, :], 0.0), writes=[zcol])

        with ExitStack() as es0:
            cxs = lambda name, shape, dt: Buf(es0.enter_context(nc.sbuf_tensor(name + "_s", list(shape), dt)), name)
            cTb = cxs("cTb", [128, KC], BF16)
            rowb = [cxs("rowb%d" % i, [1, 256], F32) for i in range(2)]
            onef = cxs("onef", [1, 1], F32)
            adb = cxs("adb", [128, 6 * KC], F32)
            cx.dma("pool", cTb.t[:, :], cT_d.t[:, :], reads=[cT_d], writes=[cTb])
            cx.dma("sp", adb.t[:, :], ada_bT.t[:, :], reads=[ada_bT], writes=[adb])
            cx.op("dve", lambda e: e.memset(onef.t[:, :], 1.0), writes=[onef])
            mp = bank()
            m0s = [cxs("m0s%d" % i, [128, WS], BF16) for i in range(6)]
            for blk in range(6 * D // 256):
                s = m0s[blk % 6]
                wv = s.t[:, 0:KC * 256].rearrange("p (k n) -> p k n", n=256)
                cx.dma("pool", wv, ada_w.t[:, blk * 256:(blk + 1) * 256].rearrange("(k p) n -> p k n", p=128),
                       reads=[ada_w], writes=[s])
                pb = bank()
                if pb is mp:
                    pb = bank()
                for k in range(KC):
                    cx.op("pe", lambda e, k=k, pb=pb, wv=wv: e.matmul(pb.t[0:1, 0:256], lhsT=cTb.t[:, k:k + 1],
                                                                     rhs=wv[:, k, :], start=(k == 0), stop=(k == KC - 1)),
                          reads=[s, cTb], writes=[pb])
                rb = rowb[blk % 2]
                cx.op("act", lambda e, pb=pb, rb=rb: e.activation(out=rb.t[0:1, :], in_=pb.t[0:1, 0:256], func=AF.Identity),
                      reads=[pb], writes=[rb])
                for j in range(2):
                    col = blk * 2 + j
                    cx.op("pe", lambda e, j=j, col=col, rb=rb: e.matmul(mp.t[:, col:col + 1],
                                                                      lhsT=rb.t[0:1, j * 128:(j + 1) * 128],
                                                                      rhs=onef.t[0:1, 0:1], start=True, stop=True),
                          reads=[rb, onef], writes=[mp])
            cx.op("dve", lambda e: e.tensor_tensor(out=modT.t[:, :], in0=mp.t[:, 0:6 * KC], in1=adb.t[:, :], op=ALU.add),
                  reads=[mp, adb], writes=[modT])
            cx.dma("sp", modT_o.t[:, :], modT.t[:, :], reads=[modT], writes=[modT_o], sembuf=modT)
            cx.op("dve", lambda e: e.scalar_tensor_tensor(out=G1.t[:, :], in0=modT.t[:, KC:2 * KC], scalar=1.0,
                                                         in1=n1g.t[:, :], op0=ALU.add, op1=ALU.mult),
                  reads=[modT, n1g], writes=[G1])
            cx.barrier()

        with ExitStack() as es1:
            cxs = lambda name, shape, dt: Buf(es1.enter_context(nc.sbuf_tensor(name + "_s", list(shape), dt)), name)
            xT = cxs("xT", [128, KC, TT], BF16)
            hT = cxs("hT", [128, KC, TT], BF16)
            sqb = [cxs("sqb%d" % i, [128, TT], BF16) for i in range(2)]
            rstd = cxs("rstd", [128, TT], F32)
            tmpf = [cxs("tmpf%d" % i, [128, TT], F32) for i in range(3)]
            obf = [cxs("obf%d" % i, [128, 512], BF16) for i in range(3)]
            of32 = [cxs("of32%d" % i, [128, 512], F32) for i in range(2)]
            cnt = {"t": 0, "o": 0, "f": 0}
            st["slots"] = wslot + [cxs("m1s%d" % i, [128, WS], BF16) for i in range(2)]
            for ti in range(L // TT):
                t0 = ti * TT
                n = TT
                cx.dma("pool", xT.t[:, :, 0:n], xbT.t[:, t0:t0 + n].rearrange("(c p) t -> p c t", p=128), reads=[xbT],
                       writes=[xT])
                ssb = bank()
                for c in range(KC):
                    sq = sqb[c % 2]
                    cx.op("act", lambda e, c=c, sq=sq: e.activation(out=sq.t[:, :n], in_=xT.t[:, c, :n], func=AF.Square),
                          reads=[xT], writes=[sq])
                    cx.op("pe", lambda e, c=c, sq=sq: e.matmul(ssb.t[:, :n], lhsT=onesb.t[:, :], rhs=sq.t[:, :n],
                                                             start=(c == 0), stop=(c == KC - 1)),
                          reads=[onesb, sq], writes=[ssb])
                cx.op("act", lambda e: e.activation(out=rstd.t[:, :n], in_=ssb.t[:, :n], func=AF.Sqrt,
                                                    bias=epsb.t[:, 0:1], scale=1.0 / D), reads=[ssb, epsb], writes=[rstd])
                cx.op("dve", lambda e: e.reciprocal(out=rstd.t[:, :n], in_=rstd.t[:, :n]), reads=[rstd], writes=[rstd])
                for c in range(KC):
                    t = tmpf[cnt["t"] % 3]
                    cnt["t"] += 1
                    cx.op("dve", lambda e, c=c, t=t: e.tensor_tensor(out=t.t[:, :n], in0=xT.t[:, c, :n], in1=rstd.t[:, :n],
                                                                   op=ALU.mult), reads=[xT, rstd], writes=[t])
                    cx.op("act", lambda e, c=c, t=t: e.activation(out=hT.t[:, c, :n], in_=t.t[:, :n], func=AF.Identity,
                                                                  bias=modT.t[:, c:c + 1], scale=G1.t[:, c:c + 1]),
                          reads=[t, modT, G1], writes=[hT])
                for (wb, dst, width, isf32) in ((w_q, QT, HW, False), (w_k, KT, HW, False), (w_u, UT, UW, True)):
                    for n0 in range(0, width, 256):
                        ncl = min(256, width - n0)
                        s, (wv,) = wload([(wb, 0, D, n0, ncl)])
                        for j in range(ncl // 128):
                            pb = bank()
                            for k in range(KC):
                                cx.op("pe", lambda e, k=k, j=j, pb=pb, wv=wv: e.matmul(
                                    pb.t[:, :n], lhsT=wv[:, k, j * 128:(j + 1) * 128], rhs=hT.t[:, k, :n],
                                    start=(k == 0), stop=(k == KC - 1)), reads=[s, hT], writes=[pb])
                            if isf32:
                                ob = of32[cnt["f"] % 2]
                                cnt["f"] += 1
                            else:
                                ob = obf[cnt["o"] % 3]
                                cnt["o"] += 1
                            cx.op("act", lambda e, pb=pb, ob=ob: e.activation(out=ob.t[:, :n], in_=pb.t[:, :n],
                                                                              func=AF.Identity), reads=[pb], writes=[ob])
                            r0 = n0 + j * 128
                            cx.dma("sp", dst.t[r0:r0 + 128, t0:t0 + n], ob.t[:, :n], reads=[ob], writes=[dst], sembuf=ob)
                for n0 in range(0, HW, 256):
                    ncl = min(256, HW - n0)
                    s, (wv,) = wload([(w_v, 0, D, n0, ncl)])
                    for (so, sn) in _subtiles(n):
                        pb = bank()
                        for k in range(KC):
                            cx.op("pe", lambda e, k=k, pb=pb, wv=wv, so=so, sn=sn: e.matmul(
                                pb.t[0:sn, 0:ncl], lhsT=hT.t[:, k, so:so + sn], rhs=wv[:, k, :],
                                start=(k == 0), stop=(k == KC - 1)), reads=[s, hT], writes=[pb])
                        ob = obf[cnt["o"] % 3]
                        cnt["o"] += 1
                        cx.op("act", lambda e, pb=pb, ob=ob, sn=sn: e.activation(out=ob.t[0:sn, 0:ncl], in_=pb.t[0:sn, 0:ncl],
                                                                                func=AF.Identity), reads=[pb], writes=[ob])
                        cx.dma("sp", Vd.t[t0 + so:t0 + so + sn, n0:n0 + ncl], ob.t[0:sn, 0:ncl], reads=[ob], writes=[Vd],
                               sembuf=ob)
            cx.barrier()
            st["slots"] = wslot

        if P.get("attn", True):
            _mixer_attention(cx, nc, P, dict(QT=QT, KT=KT, Vd=Vd, A_o=A_o, lam4_d=lam4_d, subg_d=subg_d,
                                             kbias_d=kbias_d, cmask_d=cmask_d, banks=banks, epsb=epsb))
        if P.get("s5", True):
            _mixer_s5(cx, nc, P, dict(UT=UT, zT_o=zT_o, arep_d=arep_d, apair_d=apair_d, Bre_d=Bre_d, Bim_d=Bim_d,
                                      Cre_d=Cre_d, Cim_d=Cim_d, dskip_d=dskip_d, iotap_d=iotap_d, iotaf_d=iotaf_d,
                                      tri_d=tri_d, banks=banks, dbg=dbg_o))
        cx.barrier()
    return nc


def _mixer_attention(cx, nc, P, T):
    L, NH = P["L"], P["NH"]
    LAM_INIT = 0.2
    NQB = L // 128
    SCALE = 64 ** -0.5
    QT, KT, Vd, A_o, banks, epsb = T["QT"], T["KT"], T["Vd"], T["A_o"], T["banks"], T["epsb"]
    with ExitStack() as es2:
        cxs = lambda name, shape, dt: Buf(es2.enter_context(nc.sbuf_tensor(name + "_s", list(shape), dt)), name)
        QTh = [cxs("QTh%d" % i, [128, L], BF16) for i in range(2)]
        KTh = [cxs("KTh%d" % i, [128, L], BF16) for i in range(2)]
        Vh = [cxs("Vh%d" % i, [128, NQB, 129], BF16) for i in range(2)]
        lam4 = cxs("lam4", [128, 256], F32)
        subg = cxs("subg", [128, 128], F32)
        kbias = cxs("kbias", [128, NH * NQB], F32)
        cmask = cxs("cmask", [128, 256], F32)
        pT0 = [cxs("pT0_%d" % i, [128, 256], BF16) for i in range(3)]
        pT1 = [cxs("pT1_%d" % i, [128, 256], BF16) for i in range(3)]
        pF0 = [cxs("pF0_%d" % i, [128, 256], F32) for i in range(2)]
        pF1 = [cxs("pF1_%d" % i, [128, 256], F32) for i in range(2)]
        sm = cxs("sm", [128, 16], F32)
        ltmp = cxs("ltmp", [128, 128], F32)
        of = [cxs("of%d" % i, [128, 128], F32) for i in range(2)]
        sqt = cxs("sqt", [128, 128], F32)
        rr = [cxs("rr%d" % i, [128, 8], F32) for i in range(2)]
        Ao = [cxs("Ao%d" % i, [128, 128], F32) for i in range(2)]
        for (sbt, d) in ((lam4, T["lam4_d"]), (subg, T["subg_d"]), (kbias, T["kbias_d"]), (cmask, T["cmask_d"])):
            cx.dma("sp", sbt.t[:, :], d.t[:, :], reads=[d], writes=[sbt])
        for i in range(2):
            cx.op("dve", lambda e, i=i: e.tensor_tensor(out=ltmp.t[:, 0:64], in0=lam4.t[:, i * 128:i * 128 + 64],
                                                       in1=lam4.t[:, i * 128 + 64:i * 128 + 128], op=ALU.mult),
                  reads=[lam4], writes=[ltmp])
            cx.op("dve", lambda e, i=i: e.reduce_sum(out=sm.t[:, i:i + 1], in_=ltmp.t[:, 0:64], axis=mybir.AxisListType.X),
                  reads=[ltmp], writes=[sm])
        cx.op("act", lambda e: e.activation(out=sm.t[:, 2:4], in_=sm.t[:, 0:2], func=AF.Exp), reads=[sm], writes=[sm])
        cx.op("dve", lambda e: e.tensor_tensor(out=sm.t[:, 4:5], in0=sm.t[:, 3:4], in1=sm.t[:, 2:3], op=ALU.subtract),
              reads=[sm], writes=[sm])
        cx.op("dve", lambda e: e.tensor_scalar(out=sm.t[:, 4:5], in0=sm.t[:, 4:5], scalar1=-LAM_INIT, scalar2=None,
                                               op0=ALU.add), reads=[sm], writes=[sm])
        cx.op("dve", lambda e: e.tensor_scalar(out=subg.t[:, :], in0=subg.t[:, :], scalar1=1.0 - LAM_INIT, scalar2=None,
                                               op0=ALU.mult), reads=[subg], writes=[subg])
        obanks = banks[0:4]
        pbanks = banks[4:8]
        ctr = {"p": 0, "t": 0, "f": 0, "q": 0}
        for h in range(NH):
            q_, k_, v_ = QTh[h % 2], KTh[h % 2], Vh[h % 2]
            cx.dma("sp", q_.t[:, :], QT.t[h * 128:(h + 1) * 128, :], reads=[QT], writes=[q_])
            cx.dma("sp", k_.t[:, :], KT.t[h * 128:(h + 1) * 128, :], reads=[KT], writes=[k_])
            cx.dma("sp", v_.t[:, :, 0:128], Vd.t[:, h * 128:(h + 1) * 128].rearrange("(b p) e -> p b e", p=128),
                   reads=[Vd], writes=[v_])
            cx.op("dve", lambda e, v_=v_: e.memset(v_.t[:, :, 128:129], 1.0), writes=[v_])
            for m in range(NQB // 2):
                qa, qb_ = 2 * m, 2 * m + 1
                OA1, OA2, OB1, OB2 = obanks

                def emit_scores(step):
                    pss = [pbanks[(ctr["p"] + i) % 4] for i in range(2)]
                    ctr["p"] += 2
                    items = []
                    if step >= 0:
                        items.append((qa, step, 0))
                    items.append((qb_, step + 1, 128))
                    for (qq, kk, col) in items:
                        for hf in range(2):
                            cx.op("pe", lambda e, hf=hf, qq=qq, kk=kk, col=col: e.matmul(
                                pss[hf].t[:, col:col + 128], lhsT=k_.t[hf * 64:(hf + 1) * 64, kk * 128:(kk + 1) * 128],
                                rhs=q_.t[hf * 64:(hf + 1) * 64, qq * 128:(qq + 1) * 128], start=True, stop=True),
                                reads=[k_, q_], writes=[pss[hf]])
                    return pss

                steps = list(range(-1, qa + 1))
                nxt = emit_scores(steps[0])
                for si, step in enumerate(steps):
                    pss = nxt
                    if si + 1 < len(steps):
                        nxt = emit_scores(steps[si + 1])
                    c0 = 128 if step < 0 else 0
                    diag = (step == qa)
                    pts = [pT0[ctr["t"] % 3], pT1[ctr["t"] % 3]]
                    ctr["t"] += 1
                    dlt = qb_ if step < 0 else qa - step
                    bcol = h * NQB + dlt
                    if diag:
                        dsts = [pF0[ctr["f"] % 2], pF1[ctr["f"] % 2]]
                        ctr["f"] += 1
                    else:
                        dsts = pts
                    for hf in range(2):
                        cx.op("act", lambda e, hf=hf: e.activation(
                            out=dsts[hf].t[:, c0:256], in_=pss[hf].t[:, c0:256], func=AF.Exp,
                            bias=kbias.t[:, bcol:bcol + 1], scale=SCALE), reads=[pss[hf], kbias], writes=[dsts[hf]])
                        if diag:
                            cx.op("dve", lambda e, hf=hf: e.tensor_tensor(out=pts[hf].t[:, :], in0=dsts[hf].t[:, :],
                                                                         in1=cmask.t[:, :], op=ALU.mult),
                                  reads=[dsts[hf], cmask], writes=[pts[hf]])
                    pv = []
                    if step >= 0:
                        pv += [(0, OA1, 0, step, step == 0, step == qa), (1, OA2, 0, step, step == 0, step == qa)]
                    pv += [(0, OB1, 128, step + 1, step < 0, step == qa), (1, OB2, 128, step + 1, step < 0, step == qa)]
                    for (hf, Ob, col, kk, st_, sp_) in pv:
                        cx.op("pe", lambda e, hf=hf, Ob=Ob, col=col, kk=kk, st_=st_, sp_=sp_: e.matmul(
                            Ob.t[:, 0:129], lhsT=pts[hf].t[:, col:col + 128], rhs=v_.t[:, kk, :],
                            start=st_, stop=sp_), reads=[pts[hf], v_], writes=[Ob])
                for (qq, O1, O2) in ((qa, OA1, OA2), (qb_, OB1, OB2)):
                    r = rr[qq % 2]
                    o = of[qq % 2]
                    ao = Ao[qq % 2]
                    cx.op("dve", lambda e: e.reciprocal(out=r.t[:, 0:1], in_=O1.t[:, 128:129]), reads=[O1], writes=[r])
                    cx.op("dve", lambda e: e.reciprocal(out=r.t[:, 1:2], in_=O2.t[:, 128:129]), reads=[O2], writes=[r])
                    cx.op("dve", lambda e: e.tensor_tensor(out=r.t[:, 2:3], in0=r.t[:, 1:2], in1=sm.t[:, 4:5], op=ALU.mult),
                          reads=[r, sm], writes=[r])
                    cx.op("dve", lambda e: e.tensor_scalar(out=o.t[:, :], in0=O1.t[:, 0:128], scalar1=r.t[:, 0:1],
                                                           scalar2=None, op0=ALU.mult), reads=[O1, r], writes=[o])
                    cx.op("dve", lambda e: e.scalar_tensor_tensor(out=o.t[:, :], in0=O2.t[:, 0:128], scalar=r.t[:, 2:3],
                                                                  in1=o.t[:, :], op0=ALU.mult, op1=ALU.add),
                          reads=[O2, r, o], writes=[o])
                    cx.op("dve", lambda e: e.tensor_tensor(out=sqt.t[:, :], in0=o.t[:, :], in1=o.t[:, :], op=ALU.mult),
                          reads=[o], writes=[sqt])
                    cx.op("dve", lambda e: e.reduce_sum(out=r.t[:, 3:4], in_=sqt.t[:, :], axis=mybir.AxisListType.X),
                          reads=[sqt], writes=[r])
                    cx.op("act", lambda e: e.activation(out=r.t[:, 4:5], in_=r.t[:, 3:4], func=AF.Sqrt,
                                                        bias=epsb.t[:, 0:1], scale=1.0 / 128), reads=[r, epsb], writes=[r])
                    cx.op("dve", lambda e: e.reciprocal(out=r.t[:, 5:6], in_=r.t[:, 4:5]), reads=[r], writes=[r])
                    cx.op("dve", lambda e: e.scalar_tensor_tensor(
                        out=ao.t[:, :], in0=o.t[:, :], scalar=r.t[:, 5:6], in1=subg.t[:, :], op0=ALU.mult, op1=ALU.mult),
                        reads=[o, r, subg], writes=[ao])
                    cx.dma("sp", A_o.t[qq * 128:(qq + 1) * 128, h * 128:(h + 1) * 128], ao.t[:, :], reads=[ao], writes=[A_o],
                           sembuf=ao)
        cx.barrier()


def _mixer_s5(cx, nc, P, T):
    L, NG = P["L"], P["NG"]
    UC, NP_, GP = NG // 8, NG // 2, NG * 64
    NB = L // 128
    PI = float(np.pi)
    X = mybir.AxisListType.X
    UT, zT_o, banks = T["UT"], T["zT_o"], T["banks"]
    with ExitStack() as es3:
        cxs = lambda name, shape, dt: Buf(es3.enter_context(nc.sbuf_tensor(name + "_s", list(shape), dt)), name)
        Emre = cxs("Emre", [128, GP], F32)
        Emim = cxs("Emim", [128, GP], F32)
        Epre = cxs("Epre", [128, NP_, 128], F32)
        Epim = cxs("Epim", [128, NP_, 128], F32)
        Bbre = cxs("Bbre", [128, GP], BF16)
        Bbim = cxs("Bbim", [128, GP], BF16)
        Cre = cxs("Cre", [128, NP_ * 128], F32)
        Cimn = cxs("Cimn", [128, NP_ * 128], F32)
        trib = cxs("trib", [128, 128], BF16)
        iotap = cxs("iotap", [128, 2], F32)
        iotaf = cxs("iotaf", [128, 128], F32)
        dskip = cxs("dskip", [128, UC], F32)
        abp = cxs("abp", [128, 2, NP_], F32)
        csb = [cxs("cs%d" % i, [128, 2, NP_], F32) for i in range(2)]
        pp = cxs("pp", [128, 12, NP_], F32)
        cx.dma("pool", trib.t[:, :], T["tri_d"].t[:, :], reads=[T["tri_d"]], writes=[trib])
        cx.dma("sp", iotap.t[:, 0:1], T["iotap_d"].t[:, :], reads=[T["iotap_d"]], writes=[iotap])
        cx.dma("sp", iotaf.t[:, :], T["iotaf_d"].t[:, :], reads=[T["iotaf_d"]], writes=[iotaf])
        cx.dma("sp", dskip.t[:, :], T["dskip_d"].t[:, :], reads=[T["dskip_d"]], writes=[dskip])
        cx.dma("sp", Cre.t[:, :], T["Cre_d"].t[:, :], reads=[T["Cre_d"]], writes=[Cre])
        cx.dma("sp", Cimn.t[:, :], T["Cim_d"].t[:, :], reads=[T["Cim_d"]], writes=[Cimn])
        cx.op("dve", lambda e: e.tensor_scalar(out=Cimn.t[:, :], in0=Cimn.t[:, :], scalar1=-1.0, scalar2=None, op0=ALU.mult),
              reads=[Cimn], writes=[Cimn])
        cx.op("dve", lambda e: e.tensor_scalar(out=iotap.t[:, 1:2], in0=iotap.t[:, 0:1], scalar1=-1.0, scalar2=None,
                                               op0=ALU.mult), reads=[iotap], writes=[iotap])

        I32 = mybir.dt.int32

        def sincos(out_sin, out_cos, arg_ap, rbufs, wb, tA, tF, tI, tbufs):
            for (dst, shift) in ((out_sin, 0.0), (out_cos, 0.5 * PI)):
                src = arg_ap
                if shift != 0.0:
                    cx.op("dve", lambda e: e.tensor_scalar(out=tA, in0=arg_ap, scalar1=shift, scalar2=None, op0=ALU.add),
                          reads=rbufs, writes=tbufs)
                    src = tA
                cx.op("dve", lambda e: e.tensor_scalar(out=tI, in0=src, scalar1=1.0 / (2 * PI), scalar2=None, op0=ALU.mult),
                      reads=rbufs + tbufs, writes=tbufs)
                cx.op("dve", lambda e: e.tensor_copy(out=tF, in_=tI), reads=tbufs, writes=tbufs)
                cx.op("dve", lambda e: e.scalar_tensor_tensor(out=tF, in0=tF, scalar=-2 * PI, in1=src, op0=ALU.mult,
                                                             op1=ALU.add), reads=rbufs + tbufs, writes=tbufs)
                cx.op("dve", lambda e: e.tensor_scalar(out=tF, in0=tF, scalar1=-PI, scalar2=PI, op0=ALU.max, op1=ALU.min),
                      reads=tbufs, writes=tbufs)
                cx.op("act", lambda e: e.activation(out=dst, in_=tF, func=AF.Sin), reads=tbufs, writes=[wb])

        with ExitStack() as es4:
            cxt = lambda name, shape, dt: Buf(es4.enter_context(nc.sbuf_tensor(name + "_s", list(shape), dt)), name)
            arep = cxt("arep", [128, 3, GP], F32)
            HC = min(GP, 1024)
            wk = cxt("wk", [128, 9, HC], F32)
            wki = cxt("wki", [128, HC], I32)
            Braw = cxt("Braw", [128, 2, GP], F32)
            cx.dma("sp", arep.t[:, :, :], T["arep_d"].t[:, :].rearrange("p (s n) -> p s n", s=3), reads=[T["arep_d"]],
                   writes=[arep])
            cx.dma("sp", Braw.t[:, 0, :], T["Bre_d"].t[:, :], reads=[T["Bre_d"]], writes=[Braw])
            cx.dma("sp", Braw.t[:, 1, :], T["Bim_d"].t[:, :], reads=[T["Bim_d"]], writes=[Braw])
            for hf_ in range(GP // HC):
                cs = slice(hf_ * HC, (hf_ + 1) * HC)
                cx.op("act", lambda e: e.activation(out=wk.t[:, 0, :], in_=arep.t[:, 2, cs], func=AF.Exp), reads=[arep], writes=[wk])
                cx.op("dve", lambda e: e.tensor_tensor(out=wk.t[:, 1, :], in0=arep.t[:, 0, cs], in1=wk.t[:, 0, :], op=ALU.mult),
                      reads=[arep, wk], writes=[wk])
                cx.op("dve", lambda e: e.tensor_tensor(out=wk.t[:, 2, :], in0=arep.t[:, 1, cs], in1=wk.t[:, 0, :], op=ALU.mult),
                      reads=[arep, wk], writes=[wk])
                cx.op("act", lambda e: e.activation(out=wk.t[:, 3, :], in_=wk.t[:, 1, :], func=AF.Exp, scale=iotap.t[:, 1:2]),
                      reads=[wk, iotap], writes=[wk])
                cx.op("dve", lambda e: e.tensor_scalar(out=wk.t[:, 4, :], in0=wk.t[:, 2, :], scalar1=iotap.t[:, 0:1],
                                                       scalar2=None, op0=ALU.mult), reads=[wk, iotap], writes=[wk])
                sincos(wk.t[:, 5, :], wk.t[:, 6, :], wk.t[:, 4, :], [wk], wk, wk.t[:, 7, :], wk.t[:, 8, :], wki.t[:, :], [wk, wki])
                cx.op("dve", lambda e: e.tensor_tensor(out=Emre.t[:, cs], in0=wk.t[:, 3, :], in1=wk.t[:, 6, :], op=ALU.mult),
                      reads=[wk], writes=[Emre])
                cx.op("dve", lambda e: e.scalar_tensor_tensor(out=Emim.t[:, cs], in0=wk.t[:, 3, :], scalar=-1.0, in1=wk.t[:, 5, :],
                                                             op0=ALU.mult, op1=ALU.mult), reads=[wk], writes=[Emim])
                sincos(wk.t[:, 5, :], wk.t[:, 6, :], wk.t[:, 2, :], [wk], wk, wk.t[:, 7, :], wk.t[:, 8, :], wki.t[:, :], [wk, wki])
                cx.op("act", lambda e: e.activation(out=wk.t[:, 0, :], in_=wk.t[:, 1, :], func=AF.Exp), reads=[wk], writes=[wk])
                cx.op("dve", lambda e: e.tensor_tensor(out=wk.t[:, 3, :], in0=wk.t[:, 0, :], in1=wk.t[:, 6, :], op=ALU.mult),
                      reads=[wk], writes=[wk])
                cx.op("dve", lambda e: e.tensor_tensor(out=wk.t[:, 4, :], in0=wk.t[:, 0, :], in1=wk.t[:, 5, :], op=ALU.mult),
                      reads=[wk], writes=[wk])
                cx.op("dve", lambda e: e.tensor_scalar(out=wk.t[:, 3, :], in0=wk.t[:, 3, :], scalar1=-1.0, scalar2=None,
                                                       op0=ALU.add), reads=[wk], writes=[wk])
                cx.op("dve", lambda e: e.tensor_tensor(out=wk.t[:, 0, :], in0=arep.t[:, 0, cs], in1=arep.t[:, 0, cs], op=ALU.mult),
                      reads=[arep], writes=[wk])
                cx.op("dve", lambda e: e.tensor_tensor(out=wk.t[:, 1, :], in0=arep.t[:, 1, cs], in1=arep.t[:, 1, cs], op=ALU.mult),
                      reads=[arep], writes=[wk])
                cx.op("dve", lambda e: e.tensor_tensor(out=wk.t[:, 0, :], in0=wk.t[:, 0, :], in1=wk.t[:, 1, :], op=ALU.add),
                      reads=[wk], writes=[wk])
                cx.op("dve", lambda e: e.reciprocal(out=wk.t[:, 0, :], in_=wk.t[:, 0, :]), reads=[wk], writes=[wk])
                cx.op("dve", lambda e: e.tensor_tensor(out=wk.t[:, 1, :], in0=wk.t[:, 3, :], in1=arep.t[:, 0, cs], op=ALU.mult),
                      reads=[wk, arep], writes=[wk])
                cx.op("dve", lambda e: e.tensor_tensor(out=wk.t[:, 2, :], in0=wk.t[:, 4, :], in1=arep.t[:, 1, cs], op=ALU.mult),
                      reads=[wk, arep], writes=[wk])
                cx.op("dve", lambda e: e.tensor_tensor(out=wk.t[:, 1, :], in0=wk.t[:, 1, :], in1=wk.t[:, 2, :], op=ALU.add),
                      reads=[wk], writes=[wk])
                cx.op("dve", lambda e: e.tensor_tensor(out=wk.t[:, 5, :], in0=wk.t[:, 1, :], in1=wk.t[:, 0, :], op=ALU.mult),
                      reads=[wk], writes=[wk])
                cx.op("dve", lambda e: e.tensor_tensor(out=wk.t[:, 1, :], in0=wk.t[:, 4, :], in1=arep.t[:, 0, cs], op=ALU.mult),
                      reads=[wk, arep], writes=[wk])
                cx.op("dve", lambda e: e.tensor_tensor(out=wk.t[:, 2, :], in0=wk.t[:, 3, :], in1=arep.t[:, 1, cs], op=ALU.mult),
                      reads=[wk, arep], writes=[wk])
                cx.op("dve", lambda e: e.tensor_tensor(out=wk.t[:, 1, :], in0=wk.t[:, 1, :], in1=wk.t[:, 2, :], op=ALU.subtract),
                      reads=[wk], writes=[wk])
                cx.op("dve", lambda e: e.tensor_tensor(out=wk.t[:, 6, :], in0=wk.t[:, 1, :], in1=wk.t[:, 0, :], op=ALU.mult),
                      reads=[wk], writes=[wk])
                cx.op("dve", lambda e: e.tensor_tensor(out=wk.t[:, 1, :], in0=wk.t[:, 5, :], in1=Braw.t[:, 0, cs], op=ALU.mult),
                      reads=[wk, Braw], writes=[wk])
                cx.op("dve", lambda e: e.tensor_tensor(out=wk.t[:, 2, :], in0=wk.t[:, 6, :], in1=Braw.t[:, 1, cs], op=ALU.mult),
                      reads=[wk, Braw], writes=[wk])
                cx.op("dve", lambda e: e.tensor_tensor(out=Bbre.t[:, cs], in0=wk.t[:, 1, :], in1=wk.t[:, 2, :], op=ALU.subtract),
                      reads=[wk], writes=[Bbre])
                cx.op("dve", lambda e: e.tensor_tensor(out=wk.t[:, 1, :], in0=wk.t[:, 5, :], in1=Braw.t[:, 1, cs], op=ALU.mult),
                      reads=[wk, Braw], writes=[wk])
                cx.op("dve", lambda e: e.tensor_tensor(out=wk.t[:, 2, :], in0=wk.t[:, 6, :], in1=Braw.t[:, 0, cs], op=ALU.mult),
                      reads=[wk, Braw], writes=[wk])
                cx.op("dve", lambda e: e.tensor_tensor(out=Bbim.t[:, cs], in0=wk.t[:, 1, :], in1=wk.t[:, 2, :], op=ALU.add),
                      reads=[wk], writes=[Bbim])
            cx.dma("sp", pp.t[:, 0:3, :], T["apair_d"].t[:, :].rearrange("p (s n) -> p s n", s=3), reads=[T["apair_d"]],
                   writes=[pp])
            cx.op("act", lambda e: e.activation(out=pp.t[:, 5, :], in_=pp.t[:, 2, :], func=AF.Exp), reads=[pp], writes=[pp])
            cx.op("dve", lambda e: e.tensor_tensor(out=pp.t[:, 3, :], in0=pp.t[:, 0, :], in1=pp.t[:, 5, :], op=ALU.mult),
                  reads=[pp], writes=[pp])
            cx.op("dve", lambda e: e.tensor_tensor(out=pp.t[:, 4, :], in0=pp.t[:, 1, :], in1=pp.t[:, 5, :], op=ALU.mult),
                  reads=[pp], writes=[pp])
            ppi = cxt("ppi", [128, NP_], I32)
            sincos(pp.t[:, 6, :], pp.t[:, 7, :], pp.t[:, 4, :], [pp], pp, pp.t[:, 8, :], pp.t[:, 9, :], ppi.t[:, :], [pp, ppi])
            cx.op("act", lambda e: e.activation(out=pp.t[:, 5, :], in_=pp.t[:, 3, :], func=AF.Exp), reads=[pp], writes=[pp])
            cx.op("dve", lambda e: e.tensor_tensor(out=abp.t[:, 0, :], in0=pp.t[:, 5, :], in1=pp.t[:, 7, :], op=ALU.mult),
                  reads=[pp], writes=[abp])
            cx.op("dve", lambda e: e.tensor_tensor(out=abp.t[:, 1, :], in0=pp.t[:, 5, :], in1=pp.t[:, 6, :], op=ALU.mult),
                  reads=[pp], writes=[abp])
            ew = cxt("ew", [128, 6, 128], F32)
            ewi = cxt("ewi", [128, 128], I32)
            for j in range(NP_):
                cx.op("act", lambda e, j=j: e.activation(out=ew.t[:, 0, :], in_=iotaf.t[:, :], func=AF.Exp,
                                                         scale=pp.t[:, 3, j:j + 1]), reads=[iotaf, pp], writes=[ew])
                cx.op("dve", lambda e, j=j: e.tensor_scalar(out=ew.t[:, 1, :], in0=iotaf.t[:, :], scalar1=pp.t[:, 4, j:j + 1],
                                                            scalar2=None, op0=ALU.mult), reads=[iotaf, pp], writes=[ew])
                sincos(ew.t[:, 2, :], ew.t[:, 3, :], ew.t[:, 1, :], [ew], ew, ew.t[:, 4, :], ew.t[:, 5, :], ewi.t[:, :], [ew, ewi])
                cx.op("dve", lambda e, j=j: e.tensor_tensor(out=Epre.t[:, j, :], in0=ew.t[:, 0, :], in1=ew.t[:, 3, :],
                                                           op=ALU.mult), reads=[ew], writes=[Epre])
                cx.op("dve", lambda e, j=j: e.tensor_tensor(out=Epim.t[:, j, :], in0=ew.t[:, 0, :], in1=ew.t[:, 2, :],
                                                           op=ALU.mult), reads=[ew], writes=[Epim])
            cx.barrier()

        if T.get("dbg") is not None:
            dbg = T["dbg"]
            dumps = [(Emre.t[:, 0:512], Emre, 0, 512), (Emim.t[:, 0:512], Emim, 512, 512), (Epre.t[:, 0, :], Epre, 1024, 128),
                     (Epim.t[:, 0, :], Epim, 1152, 128), (abp.t[:, 0, :], abp, 1280, NP_), (abp.t[:, 1, :], abp, 1280 + NP_, NP_),
                     (Bbre.t[:, 0:512], Bbre, 1536, 512), (Bbim.t[:, 0:512], Bbim, 2048, 512)]
            for (ap_, b_, off, n_) in dumps:
                cx.dma("pool", dbg.t[:, off:off + n_], ap_, reads=[b_], writes=[dbg], sembuf=b_)
        TB = min(4, NB)
        NGRP = NB // TB
        uTf = [cxs("uTf%d" % i, [128, UC, TB * 128], F32) for i in range(2)]
        uTb = [cxs("uTb%d" % i, [128, UC, TB * 128], BF16) for i in range(2)]
        Wc = [cxs("Wc%d" % i, [128, 2, 512], BF16) for i in range(2)]
        mt = [cxs("mt%d" % i, [128, 512], F32) for i in range(8)]
        Xre_t = cxs("Xre", [128, NP_, 128], F32)
        Xim_t = cxs("Xim", [128, NP_, 128], F32)
        xrb = [Buf(Xre_t.t, "xrb%d" % j) for j in range(NP_)]
        xib = [Buf(Xim_t.t, "xib%d" % j) for j in range(NP_)]
        dt_ = [cxs("dt%d" % i, [128, 128], F32) for i in range(8)]
        et_ = [cxs("et%d" % i, [128, 128], F32) for i in range(4)]
        yT = [cxs("yT%d" % i, [128, UC, TB * 128], F32) for i in range(2)]
        gt = [cxs("gt%d" % i, [128, UC * TB * 128], F32) for i in range(2)]
        cx.op("dve", lambda e: e.memset(csb[0].t[:, :, :], 0.0), writes=[csb[0]])
        items = [(gi, blk, uc) for gi in range(NGRP) for blk in range(TB) for uc in range(UC)]
        cnt = {"d": 0}

        def stage_a(k):
            gi, blk, uc = items[k]
            par = k % 2
            uf, ub = uTf[gi % 2], uTb[gi % 2]
            t0 = gi * TB * 128
            if blk == 0 and uc == 0:
                cx.dma("sp", uf.t[:, :, :], UT.t[:, t0:t0 + TB * 128].rearrange("(u p) t -> p u t", p=128), reads=[UT],
                       writes=[uf])
                cx.dma("pool", ub.t[:, :, :], UT.t[:, t0:t0 + TB * 128].rearrange("(u p) t -> p u t", p=128), reads=[UT],
                       writes=[ub])
            o = blk * 128
            pr, pi_ = banks[2 * par], banks[2 * par + 1]
            cr, ci = banks[4 + 2 * par], banks[5 + 2 * par]
            W = Wc[par]
            for (pb, Bb) in ((pr, Bbre), (pi_, Bbim)):
                cx.op("pe", lambda e, pb=pb, Bb=Bb: e.matmul(
                    pb.t[:, 0:512], lhsT=ub.t[:, uc, o:o + 128], rhs=Bb.t[:, uc * 512:(uc + 1) * 512],
                    start=True, stop=True), reads=[ub, Bb], writes=[pb])
            cs_ = slice(uc * 512, (uc + 1) * 512)
            m = mt[4 * par:4 * par + 4]
            for (mi, pb, Em) in ((0, pr, Emre), (1, pi_, Emim), (2, pr, Emim), (3, pi_, Emre)):
                cx.op("dve", lambda e, mi=mi, pb=pb, Em=Em: e.tensor_tensor(
                    out=m[mi].t[:, :], in0=pb.t[:, 0:512], in1=Em.t[:, cs_], op=ALU.mult),
                    reads=[pb, Em], writes=[m[mi]])
            cx.op("pool", lambda e: e.tensor_tensor(out=W.t[:, 0, :], in0=m[0].t[:, :], in1=m[1].t[:, :],
                                                    op=ALU.subtract), reads=[m[0], m[1]], writes=[W])
            cx.op("pool", lambda e: e.tensor_tensor(out=W.t[:, 1, :], in0=m[2].t[:, :], in1=m[3].t[:, :],
                                                    op=ALU.add), reads=[m[2], m[3]], writes=[W])
            for jj in range(4):
                for (cbk, ri) in ((cr, 0), (ci, 1)):
                    cx.op("pe", lambda e, cbk=cbk, ri=ri, jj=jj: e.matmul(
                        cbk.t[:, jj * 128:(jj + 1) * 128], lhsT=W.t[:, ri, jj * 128:(jj + 1) * 128], rhs=trib.t[:, :],
                        start=True, stop=True), reads=[W, trib], writes=[cbk])

        def stage_b(k):
            gi, blk, uc = items[k]
            par = k % 2
            nblk = gi * TB + blk
            uf, yt = uTf[gi % 2], yT[gi % 2]
            o = blk * 128
            cr, ci = banks[4 + 2 * par], banks[5 + 2 * par]
            yb = banks[2 * par]
            cs_cur, cs_nxt = csb[nblk % 2], csb[(nblk + 1) % 2]
            for jj in range(4):
                j = uc * 4 + jj
                d = dt_[4 * (cnt["d"] % 2):4 * (cnt["d"] % 2) + 4]
                ev = et_[2 * (cnt["d"] % 2):2 * (cnt["d"] % 2) + 2]
                cnt["d"] += 1
                for (di, cbk, ri, Ep) in ((0, cr, 0, Epre), (1, ci, 1, Epim), (2, cr, 0, Epim), (3, ci, 1, Epre)):
                    cx.op("dve", lambda e, di=di, cbk=cbk, ri=ri, Ep=Ep: e.scalar_tensor_tensor(
                        out=d[di].t[:, :], in0=cbk.t[:, jj * 128:(jj + 1) * 128], scalar=cs_cur.t[:, ri, j:j + 1],
                        in1=Ep.t[:, j, :], op0=ALU.add, op1=ALU.mult), reads=[cbk, cs_cur, Ep], writes=[d[di]])
                cx.op("pool", lambda e: e.tensor_tensor(out=Xre_t.t[:, j, :], in0=d[0].t[:, :], in1=d[1].t[:, :],
                                                        op=ALU.subtract), reads=[d[0], d[1]], writes=[xrb[j]])
                cx.op("pool", lambda e: e.tensor_tensor(out=Xim_t.t[:, j, :], in0=d[2].t[:, :], in1=d[3].t[:, :],
                                                        op=ALU.add), reads=[d[2], d[3]], writes=[xib[j]])
            for jj in range(4):
                j = uc * 4 + jj
                cx.op("pe", lambda e, j=j, jj=jj: e.matmul(yb.t[:, 0:128], lhsT=Cre.t[:, j * 128:(j + 1) * 128],
                                                           rhs=Xre_t.t[:, j, :], start=(jj == 0), stop=False),
                      reads=[Cre, xrb[j]], writes=[yb])
                cx.op("pe", lambda e, j=j, jj=jj: e.matmul(yb.t[:, 0:128], lhsT=Cimn.t[:, j * 128:(j + 1) * 128],
                                                           rhs=Xim_t.t[:, j, :], start=False, stop=(jj == 3)),
                      reads=[Cimn, xib[j]], writes=[yb])
            cx.op("dve", lambda e: e.scalar_tensor_tensor(
                out=yt.t[:, uc, o:o + 128], in0=uf.t[:, uc, o:o + 128], scalar=dskip.t[:, uc:uc + 1],
                in1=yb.t[:, 0:128], op0=ALU.mult, op1=ALU.add), reads=[uf, dskip, yb], writes=[yt])
            if uc == UC - 1:
                sre, sim = Xre_t.t[:, :, 127], Xim_t.t[:, :, 127]
                cx.op("dve", lambda e: e.tensor_tensor(out=pp.t[:, 0, :], in0=abp.t[:, 0, :], in1=sre, op=ALU.mult),
                      reads=[abp] + xrb, writes=[pp])
                cx.op("dve", lambda e: e.tensor_tensor(out=pp.t[:, 1, :], in0=abp.t[:, 1, :], in1=sim, op=ALU.mult),
                      reads=[abp] + xib, writes=[pp])
                cx.op("dve", lambda e: e.tensor_tensor(out=cs_nxt.t[:, 0, :], in0=pp.t[:, 0, :], in1=pp.t[:, 1, :],
                                                       op=ALU.subtract), reads=[pp], writes=[cs_nxt])
                cx.op("dve", lambda e: e.tensor_tensor(out=pp.t[:, 2, :], in0=abp.t[:, 0, :], in1=sim, op=ALU.mult),
                      reads=[abp] + xib, writes=[pp])
                cx.op("dve", lambda e: e.tensor_tensor(out=pp.t[:, 3, :], in0=abp.t[:, 1, :], in1=sre, op=ALU.mult),
                      reads=[abp] + xrb, writes=[pp])
                cx.op("dve", lambda e: e.tensor_tensor(out=cs_nxt.t[:, 1, :], in0=pp.t[:, 2, :], in1=pp.t[:, 3, :],
                                                       op=ALU.add), reads=[pp], writes=[cs_nxt])
            if uc == UC - 1 and blk == TB - 1:
                t0 = gi * TB * 128
                yv = yt.t[:, :, :].rearrange("p u t -> p (u t)")
                g0, g1 = gt
                cx.op("act", lambda e: e.activation(out=g0.t[:, :], in_=yv, func=AF.Square), reads=[yt], writes=[g0])
                cx.op("dve", lambda e: e.tensor_scalar(out=g0.t[:, :], in0=g0.t[:, :], scalar1=0.044715, scalar2=1.0,
                                                       op0=ALU.mult, op1=ALU.add), reads=[g0], writes=[g0])
                cx.op("dve", lambda e: e.tensor_tensor(out=g0.t[:, :], in0=g0.t[:, :], in1=yv, op=ALU.mult), reads=[g0, yt],
                      writes=[g0])
                cx.op("act", lambda e: e.activation(out=g1.t[:, :], in_=g0.t[:, :], func=AF.Sigmoid,
                                                    scale=1.5957691216057308), reads=[g0], writes=[g1])
                cx.op("dve", lambda e: e.tensor_tensor(out=g1.t[:, :], in0=g1.t[:, :], in1=yv, op=ALU.mult), reads=[g1, yt],
                      writes=[g1])
                cx.dma("sp", zT_o.t[:, t0:t0 + TB * 128].rearrange("(u p) t -> p u t", p=128),
                       g1.t[:, :].rearrange("p (u t) -> p u t", u=UC), reads=[g1], writes=[zT_o], sembuf=g1)

        stage_a(0)
        for k in range(len(items)):
            if k + 1 < len(items):
                stage_a(k + 1)
            stage_b(k)
        cx.barrier()


def _vec_layout(v):
    return np.ascontiguousarray(np.asarray(v, np.float32).reshape(-1, 128).T)


def _interleave(a, b):
    K = a.shape[0]
    n = a.shape[1] // 128
    return np.ascontiguousarray(np.stack([a.reshape(K, n, 128), b.reshape(K, n, 128)], axis=2).reshape(K, n * 256))


def _rep(v):
    return np.ascontiguousarray(np.broadcast_to(np.asarray(v, np.float32).reshape(1, -1), (128, np.size(v))))


def _const_tables(NH, NQB, slopes):
    p = np.arange(128, dtype=np.float32)
    kb = np.zeros((128, NH, NQB), np.float32)
    for h in range(NH):
        for d in range(NQB):
            kb[:, h, d] = slopes[h] * (p - 64.0 - 128.0 * d)
    cm = (p[:, None] <= p[None, :]).astype(np.float32)
    return dict(kbias=kb.reshape(128, NH * NQB), cmask=np.concatenate([cm, cm], 1), ident=np.eye(128, dtype=np.float32),
                iota_p=p.reshape(128, 1).copy(), iota_f=_rep(p), tri=cm.copy())


def _s5_layout(a_re, a_im, b_re, b_im, c_re, c_im, d_skip, log_dt):
    NG = a_re.shape[0]
    UC, NP_ = NG // 8, NG // 2
    arep = np.concatenate([_rep(a_re.reshape(-1)), _rep(a_im.reshape(-1)), _rep(np.repeat(log_dt, 64))], 1)
    ap = np.zeros((128, 3, NP_), np.float32)
    for j in range(NP_):
        for g2 in range(2):
            g = 2 * j + g2
            ap[g2 * 64:(g2 + 1) * 64, 0, j] = a_re[g]
            ap[g2 * 64:(g2 + 1) * 64, 1, j] = a_im[g]
            ap[g2 * 64:(g2 + 1) * 64, 2, j] = log_dt[g]
    Bre = np.zeros((128, UC, 8, 64), np.float32)
    Bim = np.zeros((128, UC, 8, 64), np.float32)
    Cre = np.zeros((128, NP_, 128), np.float32)
    Cim = np.zeros((128, NP_, 128), np.float32)
    for g in range(NG):
        uc, g8 = g // 8, g % 8
        Bre[g8 * 16:(g8 + 1) * 16, uc, g8, :] = b_re[g].T
        Bim[g8 * 16:(g8 + 1) * 16, uc, g8, :] = b_im[g].T
        j, g2 = g // 2, g % 2
        Cre[g2 * 64:(g2 + 1) * 64, j, g8 * 16:(g8 + 1) * 16] = c_re[g].T
        Cim[g2 * 64:(g2 + 1) * 64, j, g8 * 16:(g8 + 1) * 16] = c_im[g].T
    return dict(arep=arep, apair=ap.reshape(128, 3 * NP_), Bre_big=Bre.reshape(128, -1), Bim_big=Bim.reshape(128, -1),
                Cre_pad=Cre.reshape(128, -1), Cim_pad=Cim.reshape(128, -1), dskipT=_vec_layout(d_skip))


_D, _L, _B, _F = 4096, 8192, 2, 11008
_NCORE = 8
_CACHE = {}


def _get(name, builder, P):
    if name not in _CACHE:
        _CACHE[name] = builder(P)
    return _CACHE[name]


def kernel(x, c, ada_w, ada_b, norm1_g, w_in, lq1, lk1, lq2, lk2, subln_g,
           a_re, a_im, b_re, b_im, c_re, c_im, d_skip, log_dt, w_glu,
           w_attn_br, w_ssm_br, w_out, norm2_g, w_up, conv_w, conv_b, w_down, final_g):
    f32 = lambda a: np.ascontiguousarray(np.asarray(a, dtype=np.float32))
    x, c = f32(x), f32(c)
    w_in0 = np.asarray(w_in, np.float32)[0]
    NH, NG = 4, 32
    HWc, UWc = NH * 128, NG * 16
    NQB = _L // 128
    lam4 = _rep(np.concatenate([np.asarray(v, np.float32).reshape(-1) for v in (lq1, lk1, lq2, lk2)]))
    subg = _rep(np.asarray(subln_g, np.float32)[0])
    ada_w0 = f32(np.asarray(ada_w)[0])
    ada_bT = _vec_layout(np.asarray(ada_b)[0])
    n1gT = _vec_layout(np.asarray(norm1_g)[0])
    xT_host = [np.ascontiguousarray(x[b].T) for b in range(_B)]
    in1 = []
    wcache = {}
    for i in range(_NCORE):
        b, hg = i // 4, i % 4
        if hg not in wcache:
            slopes = np.exp2(-8.0 * (np.arange(hg * NH, (hg + 1) * NH) + 1) / 16.0).astype(np.float32)
            gs = slice(hg * NG, (hg + 1) * NG)
            d = dict(
                w_q=f32(w_in0[:, hg * HWc:(hg + 1) * HWc]),
                w_k=f32(w_in0[:, 2048 + hg * HWc:2048 + (hg + 1) * HWc]),
                w_v=f32(w_in0[:, 4096 + hg * HWc:4096 + (hg + 1) * HWc]),
                w_u=f32(w_in0[:, 6144 + hg * UWc:6144 + (hg + 1) * UWc]),
            )
            d.update(_const_tables(NH, NQB, slopes))
            d.update(_s5_layout(*[np.asarray(v, np.float32)[0][gs] for v in (a_re, a_im, b_re, b_im, c_re, c_im)],
                                np.asarray(d_skip, np.float32)[0][hg * UWc:(hg + 1) * UWc],
                                np.asarray(log_dt, np.float32)[0][gs]))
            wcache[hg] = d
        m = dict(xbT=xT_host[b], cT=_vec_layout(c[b]), ada_w=ada_w0, ada_bT=ada_bT, n1g=n1gT, lam4=lam4, subg=subg)
        m.update(wcache[hg])
        in1.append(m)
    nc1 = _get("mixer", build_mixer, dict(D=_D, L=_L, NH=NH, NG=NG))
    r1 = run_bass_kernel_spmd(nc1, in1, core_ids=list(range(_NCORE))).results
    del in1, wcache
    AT = [np.concatenate([np.asarray(r1[b * 4 + hg]["A_out"]).T for hg in range(4)], 0) for b in range(_B)]
    ZT = [np.concatenate([np.asarray(r1[b * 4 + hg]["zT"]) for hg in range(4)], 0) for b in range(_B)]
    modT = [np.asarray(r1[b * 4]["modT"]) for b in range(_B)]
    del r1
    TOKC = _L // 4
    n2gT = _vec_layout(np.asarray(norm2_g)[0])
    fgT = _vec_layout(np.asarray(final_g))
    cwl = np.concatenate([_vec_layout(np.asarray(conv_w, np.float32)[0][j]) for j in range(3)], 1)
    w_up0 = np.asarray(w_up, np.float32)[0]
    shared = dict(w_gate=_interleave(w_in0[:, 8192:12288], w_in0[:, 12288:16384]), w_glu=f32(np.asarray(w_glu)[0]), w_a=f32(np.asarray(w_attn_br)[0]),
                  w_s=f32(np.asarray(w_ssm_br)[0]), w_out=f32(np.asarray(w_out)[0]), w_up=_interleave(w_up0[:, :_F], w_up0[:, _F:]),
                  cw=cwl, cb=_vec_layout(np.asarray(conv_b)[0]), w_down=f32(np.asarray(w_down)[0]),
                  ident=np.eye(128, dtype=np.float32))
    in2 = []
    for i in range(_NCORE):
        b, j = i // 4, i % 4
        lo = j * TOKC - 2

        def halo(a, axis):
            if lo >= 0:
                return f32(np.take(a, np.arange(lo, lo + TOKC + 2), axis=axis))
            body = np.take(a, np.arange(0, TOKC), axis=axis)
            shp = list(body.shape)
            shp[axis] = 2
            return f32(np.concatenate([np.zeros(shp, np.float32), body], axis))
        m = dict(xsT=halo(xT_host[b], 1), AT=halo(AT[b], 1), ZT=halo(ZT[b], 1),
                 vecs=np.ascontiguousarray(np.concatenate([modT[b], n1gT, n2gT, fgT], 1)),
                 hmask=np.full((128, 1), 1.0 if j > 0 else 0.0, np.float32))
        m.update(shared)
        in2.append(m)
    nc2 = _get("tail", build_tail, dict(D=_D, F=_F, AW=2048, SW=2048, TOK=TOKC + 2, TT=410, FG=22))
    r2 = run_bass_kernel_spmd(nc2, in2, core_ids=list(range(_NCORE))).results
    out = np.empty((_B, _L, _D), np.float32)
    for i in range(_NCORE):
        b, j = i // 4, i % 4
        out[b, j * TOKC:(j + 1) * TOKC] = np.asarray(r2[i]["outT"]).T
    return out
```
